# Optimizing a Trainium2 kernel written in Bass

```python
import math
import jax, jax.numpy as jnp
from jax import lax
import numpy as np

D_MODEL = 1024
BATCH = 16
SEQ = 2048
DEPTH = 1

ATTN_HEADS = 4
HEAD_DIM = 64
V_HEAD_DIM = 2 * HEAD_DIM
ATTN_WIDTH = ATTN_HEADS * V_HEAD_DIM
QK_WIDTH = ATTN_HEADS * 2 * HEAD_DIM
ROT_DIM = HEAD_DIM // 4
ROPE_THETA = 500000.0
Q_BLOCK = 128
LAMBDA_INIT_SCALE = 0.1

SSM_WIDTH = D_MODEL // 2
SSM_GROUP = 16
SSM_GROUPS = SSM_WIDTH // SSM_GROUP
SSM_STATE = 64
DT_MIN = 1e-3
DT_MAX = 1e-1

EPS = 1e-6

IN_SIZES = (QK_WIDTH, QK_WIDTH, ATTN_WIDTH, ATTN_WIDTH, SSM_WIDTH, SSM_WIDTH, D_MODEL, D_MODEL)
IN_WIDTH = sum(IN_SIZES)
IN_OFFSETS = tuple(int(o) for o in np.cumsum(IN_SIZES)[:-1])

kernel_name = "hybrid_diffattn_s5_encoder_block"


def rmsnorm(x, g):
    xf = x.astype(jnp.float32)
    inv = lax.rsqrt(jnp.mean(xf * xf, axis=-1, keepdims=True) + EPS)
    return (xf * inv).astype(x.dtype) * g


def partial_rope(t, positions):
    half = ROT_DIM // 2
    inv_freq = ROPE_THETA ** (-jnp.arange(half, dtype=jnp.float32) * 2.0 / ROT_DIM)
    ang = positions.astype(jnp.float32)[:, :, None] * inv_freq
    cos = jnp.cos(ang)[:, :, None, None, :].astype(t.dtype)
    sin = jnp.sin(ang)[:, :, None, None, :].astype(t.dtype)
    t1 = t[..., :half]
    t2 = t[..., half:ROT_DIM]
    rest = t[..., ROT_DIM:]
    return jnp.concatenate([t1 * cos - t2 * sin, t2 * cos + t1 * sin, rest], axis=-1)


def diff_attention(q, k, v, lam):
    b, s = q.shape[0], q.shape[1]
    nblk = s // Q_BLOCK
    scale = HEAD_DIM ** -0.5
    qb = q.reshape(b, nblk, Q_BLOCK, ATTN_HEADS, 2, HEAD_DIM).transpose(1, 0, 2, 3, 4, 5)

    def block(qblk):
        sc = jnp.einsum('bqhcd,bkhcd->bhcqk', qblk, k).astype(jnp.float32) * scale
        p = jax.nn.softmax(sc, axis=-1)
        w = p[:, :, 0] - lam * p[:, :, 1]
        return jnp.einsum('bhqk,bkhe->bqhe', w.astype(v.dtype), v)

    o = lax.map(block, qb)
    return o.transpose(1, 0, 2, 3, 4).reshape(b, s, ATTN_HEADS, V_HEAD_DIM)


def s5_direction(u, lam_re, lam_im, log_dt, b_re, b_im, c_re, c_im, reverse):
    f32 = jnp.float32
    dt = jnp.exp(log_dt.astype(f32))[:, None]
    lr = jnp.minimum(lam_re.astype(f32), -1e-4)
    li = lam_im.astype(f32)
    mag = jnp.exp(lr * dt)
    ab_re = mag * jnp.cos(li * dt)
    ab_im = mag * jnp.sin(li * dt)
    den = lr * lr + li * li
    nr = ab_re - 1.0
    ni = ab_im
    coef_re = (nr * lr + ni * li) / den
    coef_im = (ni * lr - nr * li) / den
    bf_re = b_re.astype(f32)
    bf_im = b_im.astype(f32)
    bb_re = coef_re[..., None] * bf_re - coef_im[..., None] * bf_im
    bb_im = coef_re[..., None] * bf_im + coef_im[..., None] * bf_re
    uf = u.astype(f32)
    bu_re = jnp.einsum('bsgh,gph->sbgp', uf, bb_re)
    bu_im = jnp.einsum('bsgh,gph->sbgp', uf, bb_im)
    s = u.shape[1]
    a_re = jnp.broadcast_to(ab_re[None, None], (s, 1) + ab_re.shape)
    a_im = jnp.broadcast_to(ab_im[None, None], (s, 1) + ab_im.shape)

    def combine(e_i, e_j):
        ar_i, ai_i, br_i, bi_i = e_i
        ar_j, ai_j, br_j, bi_j = e_j
        return (ar_j * ar_i - ai_j * ai_i,
                ar_j * ai_i + ai_j * ar_i,
                ar_j * br_i - ai_j * bi_i + br_j,
                ar_j * bi_i + ai_j * br_i + bi_j)

    _, _, x_re, x_im = lax.associative_scan(combine, (a_re, a_im, bu_re, bu_im),
                                            reverse=reverse, axis=0)
    y = (jnp.einsum('sbgp,ghp->bsgh', x_re, c_re.astype(f32))
         - jnp.einsum('sbgp,ghp->bsgh', x_im, c_im.astype(f32)))
    return y.astype(u.dtype)


def hybrid_layer(x, c, positions, layer_idx, w_ada, b_ada, g_pre, w_in, lam_qk, g_subln,
                 lam_re, lam_im, log_dt, b_re, b_im, c_re, c_im, d_skip,
                 w_glu, b_glu, w_up_attn, w_up_ssm, w_out):
    b, s, _ = x.shape
    mod = jax.nn.silu(c) @ w_ada + b_ada
    shift, scale, gate = jnp.split(mod, 3, axis=-1)
    h = rmsnorm(x, g_pre) * (1.0 + scale[:, None, :]) + shift[:, None, :]

    proj = h @ w_in
    q, k, v, z_a, u, z_s, g_a, g_s = jnp.split(proj, IN_OFFSETS, axis=-1)

    q = partial_rope(q.reshape(b, s, ATTN_HEADS, 2, HEAD_DIM), positions)
    k = partial_rope(k.reshape(b, s, ATTN_HEADS, 2, HEAD_DIM), positions)
    v = v.reshape(b, s, ATTN_HEADS, V_HEAD_DIM)
    lam_init = 0.8 - 0.6 * math.exp(-0.3 * layer_idx)
    lf = lam_qk.astype(jnp.float32)
    lam = jnp.exp(jnp.sum(lf[0] * lf[1])) - jnp.exp(jnp.sum(lf[2] * lf[3])) + lam_init
    o = diff_attention(q, k, v, lam)
    o = rmsnorm(o, g_subln) * (1.0 - lam_init)
    attn_branch = (o.reshape(b, s, ATTN_WIDTH) * jax.nn.silu(z_a)) @ w_up_attn

    ug = u.reshape(b, s, SSM_GROUPS, SSM_GROUP)
    y = (s5_direction(ug, lam_re[0], lam_im[0], log_dt[0], b_re[0], b_im[0], c_re[0], c_im[0], False)
         + s5_direction(ug, lam_re[1], lam_im[1], log_dt[1], b_re[1], b_im[1], c_re[1], c_im[1], True))
    y = y.reshape(b, s, SSM_WIDTH) + d_skip * u
    y = jax.nn.gelu(y)
    y = y * jax.nn.sigmoid(y @ w_glu + b_glu)
    ssm_branch = (y * jax.nn.silu(z_s)) @ w_up_ssm

    merged = jax.nn.sigmoid(g_a) * attn_branch + jax.nn.sigmoid(g_s) * ssm_branch
    return x + gate[:, None, :] * (merged @ w_out)


def setup_inputs(seed: int = 0) -> dict:
    key = jax.random.key(seed)
    ks = jax.random.split(key, 24)
    L, D = DEPTH, D_MODEL
    G, P, Hg = SSM_GROUPS, SSM_STATE, SSM_GROUP
    nrm = jax.random.normal
    f32 = jnp.float32
    x = nrm(ks[0], (BATCH, SEQ, D), f32)
    c = nrm(ks[1], (BATCH, D), f32)
    positions = jnp.broadcast_to(jnp.arange(SEQ, dtype=jnp.int32)[None, :], (BATCH, SEQ))
    w_ada = nrm(ks[2], (L, D, 3 * D), f32) * D ** -0.5
    b_ada = nrm(ks[3], (L, 3 * D), f32) * 0.01
    g_pre = 1.0 + 0.02 * nrm(ks[4], (L, D), f32)
    w_in = nrm(ks[5], (L, D, IN_WIDTH), f32) * D ** -0.5
    lam_qk = nrm(ks[6], (L, 4, HEAD_DIM), f32) * LAMBDA_INIT_SCALE
    g_subln = 1.0 + 0.02 * nrm(ks[7], (L, V_HEAD_DIM), f32)
    n_idx = jnp.arange(P, dtype=f32)
    ssm_lam_re = -0.5 + 0.01 * nrm(ks[8], (L, 2, G, P), f32)
    ssm_lam_im = math.pi * n_idx + 0.01 * nrm(ks[9], (L, 2, G, P), f32)
    ssm_log_dt = jax.random.uniform(ks[10], (L, 2, G), f32, math.log(DT_MIN), math.log(DT_MAX))
    ssm_b_re = nrm(ks[11], (L, 2, G, P, Hg), f32) * (2 * Hg) ** -0.5
    ssm_b_im = nrm(ks[12], (L, 2, G, P, Hg), f32) * (2 * Hg) ** -0.5
    ssm_c_re = nrm(ks[13], (L, 2, G, Hg, P), f32) * (2 * P) ** -0.5
    ssm_c_im = nrm(ks[14], (L, 2, G, Hg, P), f32) * (2 * P) ** -0.5
    ssm_d = nrm(ks[15], (L, SSM_WIDTH), f32)
    w_glu = nrm(ks[16], (L, SSM_WIDTH, SSM_WIDTH), f32) * SSM_WIDTH ** -0.5
    b_glu = nrm(ks[17], (L, SSM_WIDTH), f32) * 0.01
    w_up_attn = nrm(ks[18], (L, ATTN_WIDTH, D), f32) * ATTN_WIDTH ** -0.5
    w_up_ssm = nrm(ks[19], (L, SSM_WIDTH, D), f32) * SSM_WIDTH ** -0.5
    w_out = nrm(ks[20], (L, D, D), f32) * D ** -0.5
    g_final = 1.0 + 0.02 * nrm(ks[21], (D,), f32)
    return {"x": x, "c": c, "positions": positions, "w_ada": w_ada, "b_ada": b_ada,
            "g_pre": g_pre, "w_in": w_in, "lam_qk": lam_qk, "g_subln": g_subln,
            "ssm_lam_re": ssm_lam_re, "ssm_lam_im": ssm_lam_im, "ssm_log_dt": ssm_log_dt,
            "ssm_b_re": ssm_b_re, "ssm_b_im": ssm_b_im, "ssm_c_re": ssm_c_re, "ssm_c_im": ssm_c_im,
            "ssm_d": ssm_d, "w_glu": w_glu, "b_glu": b_glu, "w_up_attn": w_up_attn,
            "w_up_ssm": w_up_ssm, "w_out": w_out, "g_final": g_final}


def reference(x, c, positions, w_ada, b_ada, g_pre, w_in, lam_qk, g_subln,
              ssm_lam_re, ssm_lam_im, ssm_log_dt, ssm_b_re, ssm_b_im, ssm_c_re, ssm_c_im,
              ssm_d, w_glu, b_glu, w_up_attn, w_up_ssm, w_out, g_final):
    for l in range(DEPTH):
        x = hybrid_layer(x, c, positions, l, w_ada[l], b_ada[l], g_pre[l], w_in[l], lam_qk[l],
                         g_subln[l], ssm_lam_re[l], ssm_lam_im[l], ssm_log_dt[l],
                         ssm_b_re[l], ssm_b_im[l], ssm_c_re[l], ssm_c_im[l], ssm_d[l],
                         w_glu[l], b_glu[l], w_up_attn[l], w_up_ssm[l], w_out[l])
    return rmsnorm(x, g_final)
```

```python
import math
import contextlib
import numpy as np
import ml_dtypes
import concourse.bass as bass
import concourse.mybir as mybir
from concourse.bass_utils import run_bass_kernel_spmd

F32 = mybir.dt.float32
BF16 = mybir.dt.bfloat16
I32 = mybir.dt.int32
AF = mybir.ActivationFunctionType
ALU = mybir.AluOpType
AX = mybir.AxisListType

ENGS = ['tensor', 'vector', 'scalar', 'gpsimd', 'sync']
TWO_PI = 2.0 * math.pi
EPS = 1e-6
SEQ = 2048
D = 1024
NCORES = 8


class Buf:
    __slots__ = ('w', 'r', 'excl')

    def __init__(self, excl=False):
        self.w = None
        self.r = {}
        self.excl = excl


class Sched:
    def __init__(self, nc):
        self.nc = nc
        self.prog = {e: [] for e in ENGS}
        self.cnt = {}
        self.waited = {e: {} for e in ENGS}
        self.semh = {}
        self.dma_keys = {}

    def _waits(self, eng, deps):
        waits = []
        for d in deps:
            if d is None:
                continue
            key, val = d
            if eng == 'tensor' and key == ('e', 'tensor'):
                continue
            if self.waited[eng].get(key, 0) >= val:
                continue
            self.waited[eng][key] = val
            waits.append((key, val))
        return waits

    def op(self, eng, fn, deps=(), sig=True):
        waits = self._waits(eng, deps)
        h = None
        if sig:
            key = ('e', eng)
            self.cnt[key] = self.cnt.get(key, 0) + 1
            h = (key, self.cnt[key])

        def emit(e, waits=waits, fn=fn, h=h):
            for key, v in waits:
                e.wait_ge(self.semh[key], v)
            ins = fn(e)
            if h is not None:
                ins.then_inc(self.semh[h[0]], 1)
        self.prog[eng].append(emit)
        return h

    def dma(self, q, out, in_, key, deps=()):
        waits = self._waits(q, deps)
        k = ('d', key)
        self.dma_keys[k] = True
        self.cnt[k] = self.cnt.get(k, 0) + 16
        h = (k, self.cnt[k])

        def emit(e, waits=waits, h=h, out=out, in_=in_):
            for key_, v in waits:
                e.wait_ge(self.semh[key_], v)
            e.dma_start(out=out, in_=in_).then_inc(self.semh[h[0]], 16)
        self.prog[q].append(emit)
        return h

    @staticmethod
    def _deps(reads, writes):
        deps = []
        for b in reads:
            deps.append(b.w)
        for b in writes:
            deps.append(b.w)
            deps.extend(b.r.items())
        return deps

    @staticmethod
    def _mark(h, reads, writes):
        for b in reads:
            if h[1] > b.r.get(h[0], 0):
                b.r[h[0]] = h[1]
        for b in writes:
            b.w = h
            b.r = {}

    def do(self, eng, fn, reads=(), writes=()):
        ex = [b for b in reads if b.excl]
        if ex:
            reads = [b for b in reads if not b.excl]
            writes = list(writes) + ex
        h = self.op(eng, fn, self._deps(reads, writes), True)
        self._mark(h, reads, writes)
        return h

    def dmado(self, q, out, in_, key, reads=(), writes=()):
        h = self.dma(q, out, in_, key, self._deps(reads, writes))
        self._mark(h, reads, writes)
        return h

    def wait_only(self, eng, deps):
        waits = self._waits(eng, deps)

        def emit(e, waits=waits):
            for key, v in waits:
                e.wait_ge(self.semh[key], v)
        self.prog[eng].append(emit)

    def run(self):
        nc = self.nc
        keys = [('e', e) for e in ENGS] + list(self.dma_keys.keys())
        with contextlib.ExitStack() as st:
            for i, k in enumerate(keys):
                self.semh[k] = st.enter_context(nc.semaphore("s%d" % i))
            block = st.enter_context(nc.Block())

            @block.tensor
            def _(e):
                for f in self.prog['tensor']:
                    f(e)

            @block.vector
            def _(e):
                for f in self.prog['vector']:
                    f(e)

            @block.scalar
            def _(e):
                for f in self.prog['scalar']:
                    f(e)

            @block.gpsimd
            def _(e):
                for f in self.prog['gpsimd']:
                    f(e)

            @block.sync
            def _(e):
                for f in self.prog['sync']:
                    f(e)


def host_consts():
    c = {}
    c["ident_f"] = np.eye(128, dtype=np.float32)
    c["ident_b"] = np.eye(128, dtype=np.float32).astype(ml_dtypes.bfloat16)
    pm = np.zeros((128, 128), np.float32)
    for m in range(128):
        j = m % 64
        if j < 8:
            pm[m + 8, m] = -1.0
        elif j < 16:
            pm[m - 8, m] = 1.0
    c["permT"] = pm.astype(ml_dtypes.bfloat16)
    invf = np.zeros((128, 1), np.float32)
    for p in range(128):
        j = p % 64
        if j < 16:
            invf[p, 0] = np.float32(500000.0) ** np.float32(-(j % 8) * 2.0 / 16.0)
    c["invf"] = (invf / np.float32(TWO_PI)).astype(np.float32)
    R = np.zeros((128, 8, 240), np.float32)
    for a in range(8):
        for h in range(16):
            R[16 * a + h, a, 112 + h] = 1.0
    c["rtab"] = R.astype(ml_dtypes.bfloat16)
    c["iota_c"] = np.arange(256, dtype=np.float32)[None, :]
    dm = np.zeros((16, 16), np.float32)
    for h in range(16):
        dm[h, h] = 1.0
    c["dmask"] = dm
    sg = np.full((128, 1), TWO_PI, np.float32)
    sg[64:] = -TWO_PI
    c["sgn2pi"] = sg
    return c


class StopBuild(Exception):
    pass


def build(stage='all'):
    nc = bass.Bass("TRN2", target_bir_lowering=False)

    def din(name, shape, dt=F32):
        return nc.dram_tensor(name, list(shape), dt, kind="ExternalInput").ap()

    x = din("x", [2, SEQ, D])
    cT = din("cT", [D, 2])
    pos = din("pos", [2, SEQ], I32)
    w_ada = din("w_ada", [D, 3 * D])
    b_ada = din("b_ada", [1, 3 * D])
    g_pre = din("g_pre", [1, D])
    w_in = din("w_in", [D, 5120])
    lam_qk = din("lam_qk", [1, 256])
    g_subln = din("g_subln", [128, 1])
    lam_re = din("lam_re", [64, 64])
    lam_im = din("lam_im", [64, 64])
    log_dt = din("log_dt", [1, 64])
    b_re = din("b_re", [2, 32, 64, 16])
    b_im = din("b_im", [2, 32, 64, 16])
    c_re = din("c_re", [2, 32, 16, 64])
    c_im = din("c_im", [2, 32, 16, 64])
    ssm_d = din("ssm_d", [1, 512])
    w_glu = din("w_glu", [512, 512])
    b_glu = din("b_glu", [1, 512])
    w_up_attn = din("w_up_attn", [512, D])
    w_up_ssm = din("w_up_ssm", [512, D])
    w_out = din("w_out", [D, D])
    g_final = din("g_final", [1, D])
    k_ident_f = din("ident_f", [128, 128])
    k_ident_b = din("ident_b", [128, 128], BF16)
    k_permT = din("permT", [128, 128], BF16)
    k_invf = din("invf", [128, 1])
    k_rtab = din("rtab", [128, 8, 240], BF16)
    k_iota = din("iota_c", [1, 256])
    k_dmask = din("dmask", [16, 16])
    k_sgn = din("sgn2pi", [128, 1])
    out = nc.dram_tensor("out", [2, SEQ, D], F32, kind="ExternalOutput").ap()
    ssmw = nc.dram_tensor("ssmw", [32, 128, 7 * 128], BF16, kind="Internal").ap()
    dbg_out = {}
    order = ['prep', 'p0', 'rope', 'p1', 'qk', 'v', 'attn', 'yt', 'mt', 'ssm_u', 'ssm_y', 'ssm', 'all']

    def at(name):
        return stage == name

    S = Sched(nc)
    st = contextlib.ExitStack()
    with st:
        def sb(name, shape, dt=F32):
            return st.enter_context(nc.sbuf_tensor("sb_" + name, list(shape), dt))

        def psum(name, shape, dt=F32):
            return st.enter_context(nc.psum_tensor("ps_" + name, list(shape), dt))

        def dump(name, ap_, reads):
            t = nc.dram_tensor("dbg_" + name, list(ap_.shape), ap_.dtype, kind="ExternalOutput").ap()
            dbg_out[name] = t
            S.dmado('sync', t, ap_, 'dbgst', reads=reads)

        nc_ctx = st.enter_context(nc.allow_non_contiguous_dma(reason="small strided parameter loads"))

        A1 = sb("A1", [128, 16384], BF16)
        A2 = sb("A2", [128, 16384], BF16)
        A3 = sb("A3", [128, 8448], BF16)
        A4 = sb("A4", [128, 8192], BF16)
        WB = sb("WB", [128, 8192], BF16)
        bA1, bA2, bA3, bA4, bWB = Buf(), Buf(), Buf(), Buf(), Buf()

        PB = [psum("pb%d" % i, [128, 512]) for i in range(8)]
        bPB = [Buf(excl=True) for _ in range(8)]

        ident_f = sb("ident_f", [128, 128])
        ident_b = sb("ident_b", [128, 128], BF16)
        permT = sb("permT", [128, 128], BF16)
        invf = sb("invf", [128, 1])
        rtab = sb("rtab", [128, 8, 240], BF16)
        iota_c = sb("iota_c", [128, 256])
        dmask = sb("dmask", [16, 16])
        sgn2pi = sb("sgn2pi", [128, 1])
        bconst = Buf()
        for t, k in [(ident_f, k_ident_f), (ident_b, k_ident_b), (permT, k_permT), (invf, k_invf),
                     (rtab, k_rtab), (dmask, k_dmask), (sgn2pi, k_sgn)]:
            S.dmado('sync', t[:], k, 'const', writes=[bconst])
        S.dmado('sync', iota_c[:], k_iota.broadcast_to([128, 256]), 'const', writes=[bconst])
        negh = sb("negh", [128, 1])
        qtr = sb("qtr", [128, 1])
        S.do('gpsimd', lambda e: e.memset(qtr[:], 0.25), writes=[bconst])
        S.do('gpsimd', lambda e: e.memset(negh[:], -0.5), writes=[bconst])

        rho = sb("rho", [128, 64])
        phq = sb("phq", [128, 64])
        brho = Buf()
        modT = sb("modT", [128, 16, 2])
        a_col = sb("a_col", [128, 8, 2])
        s_col = sb("s_col", [128, 8, 2])
        gate_bc = sb("gate_bc", [128, 2, D])
        gfin_bc = sb("gfin_bc", [128, D])
        lamneg = sb("lamneg", [128, 1])
        gs08 = sb("gs08", [128, 1])
        bglu_c = sb("bglu_c", [128, 4])
        bmod = Buf()

        def tmp_pool(name, n, shape, dt=F32):
            return [(sb("%s%d" % (name, i), shape, dt), Buf()) for i in range(n)]

        NSTG, NWS = 2, 3
        stg = [(sb("stg%d" % i, [128, 8, 128]), Buf()) for i in range(NSTG)]
        wsl = [(sb("wsl%d" % i, [128, 8, 128], BF16), Buf()) for i in range(NWS)]
        xt_pool = tmp_pool("xt", 2, [128, D])
        xs_pool = tmp_pool("xs", 2, [128, D])

        def V_ts(eng, out_, in0, s1, s2, op0, op1=None, reads=(), writes=()):
            if op1 is None:
                return S.do(eng, lambda e: e.tensor_scalar(out=out_, in0=in0, scalar1=s1, scalar2=None, op0=op0), reads, writes)
            return S.do(eng, lambda e: e.tensor_scalar(out=out_, in0=in0, scalar1=s1, scalar2=s2, op0=op0, op1=op1), reads, writes)

        def V_tt(eng, out_, in0, in1, op, reads=(), writes=()):
            return S.do(eng, lambda e: e.tensor_tensor(out=out_, in0=in0, in1=in1, op=op), reads, writes)

        def V_cp(eng, out_, in_, reads=(), writes=()):
            return S.do(eng, lambda e: e.tensor_copy(out=out_, in_=in_), reads, writes)

        def A_act(out_, in_, func, reads=(), writes=(), **kw):
            return S.do('scalar', lambda e: e.activation(out=out_, in_=in_, func=func, **kw), reads, writes)

        def turns_to_sincos(eng, tq, sin_out, cos_out, scr, bscr, reads, writes, sin_scale=TWO_PI):
            (ti, fr, t2) = scr
            R = list(reads) + [bscr]
            V_cp(eng, ti, tq, reads=R, writes=[bscr])
            V_cp(eng, fr, ti, reads=R, writes=[bscr])
            V_tt(eng, fr, tq, fr, ALU.subtract, reads=R, writes=[bscr])
            A_act(sin_out, fr, AF.Sin, reads=R, writes=list(writes) + [bscr], scale=sin_scale)
            V_ts(eng, t2, tq, 0.25, None, ALU.add, reads=R, writes=[bscr])
            V_cp(eng, ti, t2, reads=R, writes=[bscr])
            V_cp(eng, fr, ti, reads=R, writes=[bscr])
            V_tt(eng, fr, t2, fr, ALU.subtract, reads=R, writes=[bscr])
            A_act(cos_out, fr, AF.Sin, reads=R, writes=list(writes) + [bscr], scale=TWO_PI)

        tsc_i2 = sb("tsc_i2", [128, 2, 256], I32)
        tsc_f2 = sb("tsc_f2", [128, 2, 256])
        tsc_i = tsc_i2[:, 0, :]
        tsc_f = tsc_f2[:, 0, :]
        tsc_g = sb("tsc_g", [128, 256])
        btsc = Buf()

        def ssm_prep():
            a1f = A1[:, :].bitcast(F32)
            a2f = A2[:, :].bitcast(F32)
            BRE, BIM = a1f[:, 0:1024], a1f[:, 1024:2048]
            CURR, CURI = a1f[:, 2048:3072], a1f[:, 3072:4096]
            NXR, NXI = a1f[:, 4096:5120], a1f[:, 5120:6144]
            TB1, TB2 = a1f[:, 6144:7168], a1f[:, 7168:8192]
            CRE, CIM = a2f[:, 0:1024], a2f[:, 1024:2048]
            CCR, CCI = a2f[:, 2048:3072], a2f[:, 3072:4096]
            CNR, CNI = a2f[:, 4096:5120], a2f[:, 5120:6144]
            TC1, TC2 = a2f[:, 6144:7168], a2f[:, 7168:8192]
            BLt = A4[:, :].rearrange("p (d g j h) -> p d g j h", d=2, g=32, j=8)
            WoA = A3[:, 0:8192].rearrange("p (d g j h) -> p d g j h", d=2, g=32, j=8)
            CL0 = xs_pool[1][0][:, 0:512].bitcast(BF16).rearrange("p (d g h) -> p d g h", d=2, g=32)
            Wo2A = WB[:, :].rearrange("p (d g j h) -> p d g j h", d=2, g=32, j=8)
            bB, bC, bT = Buf(), Buf(), Buf()
            small = {}

            stgflat = [stg[0][0][:, :, :].rearrange("p a b -> p (a b)"), stg[1][0][:, :, :].rearrange("p a b -> p (a b)")]

            class _T:
                def __init__(self, ap_):
                    self.ap_ = ap_

                def __getitem__(self, k):
                    return self.ap_[k]

            def sm(name):
                if name not in small:
                    i = len(small)
                    assert i < 32
                    small[name] = _T(stgflat[i // 16][:, (i % 16) * 64:(i % 16 + 1) * 64])
                return small[name]
            bs = Buf()
            lnat = wsl[2][0][0:64, :, :].rearrange("p a b -> p (a b)")[:, 0:512].bitcast(F32).rearrange("p (a u q) -> p a u q", a=2, u=2)
            for a, src in enumerate([lam_re, lam_im]):
                for dup in range(2):
                    S.dmado('sync', lnat[:, a, dup, :], src, 'prep_s', writes=[bs])
            S.dmado('sync', sm("ldt")[:], log_dt.broadcast_to([128, 64]), 'prep_s', writes=[bs])
            for dup in range(2):
                for d in range(2):
                    S.dmado('sync', BRE[64 * dup:64 * dup + 64, d * 512:(d + 1) * 512].rearrange("p (g h) -> p g h", h=16),
                            b_re[d].rearrange("g p h -> p g h"), 'prep_b', writes=[bB])
                    S.dmado('sync', BIM[64 * dup:64 * dup + 64, d * 512:(d + 1) * 512].rearrange("p (g h) -> p g h", h=16),
                            b_im[d].rearrange("g p h -> p g h"), 'prep_b', writes=[bB])
            cnr = TC1.rearrange("p (j u q) -> p j u q", j=8, u=2)
            cni = TC2.rearrange("p (j u q) -> p j u q", j=8, u=2)
            for dup in range(2):
                for d in range(2):
                    S.dmado('sync', cnr[:, d * 4:(d + 1) * 4, dup, :],
                            c_re[d].rearrange("(b g) h p -> (g h) b p", g=8), 'prep_c', writes=[bC])
                    S.dmado('sync', cni[:, d * 4:(d + 1) * 4, dup, :],
                            c_im[d].rearrange("(b g) h p -> (g h) b p", g=8), 'prep_c', writes=[bC])
            S.do('tensor', lambda e: e.transpose(PB[0][:, 0:64], lnat[:, 0, :, :].rearrange("p u q -> p (u q)"), ident_f[0:64, 0:64]),
                 reads=[bs, bconst], writes=[bPB[0]])
            V_cp('vector', sm("lre")[:], PB[0][:, 0:64], reads=[bPB[0]], writes=[bs])
            S.do('tensor', lambda e: e.transpose(PB[1][:, 0:64], lnat[:, 1, :, :].rearrange("p u q -> p (u q)"), ident_f[0:64, 0:64]),
                 reads=[bs, bconst], writes=[bPB[1]])
            V_cp('vector', sm("li")[:], PB[1][:, 0:64], reads=[bPB[1]], writes=[bs])
            R_, W_ = [bs], [bs]
            A_act(sm("dt")[:], sm("ldt")[:], AF.Exp, reads=R_, writes=W_)
            V_ts('vector', sm("lr")[:], sm("lre")[:], -1e-4, None, ALU.min, reads=R_, writes=W_)
            V_tt('vector', sm("x1")[:], sm("lr")[:], sm("dt")[:], ALU.mult, reads=R_, writes=W_)
            A_act(sm("mag")[:], sm("x1")[:], AF.Exp, reads=R_, writes=W_)
            A_act(rho[:], sm("x1")[:], AF.Exp, reads=R_, writes=[brho, bs], scale=8.0)
            V_tt('vector', sm("ang")[:], sm("li")[:], sm("dt")[:], ALU.mult, reads=R_, writes=W_)
            V_ts('vector', sm("tq")[:], sm("ang")[:], 1.0 / TWO_PI, None, ALU.mult, reads=R_, writes=W_)
            V_ts('vector', phq[:], sm("tq")[:], 8.0, None, ALU.mult, reads=R_, writes=[brho, bs])
            turns_to_sincos('vector', sm("tq")[:], sm("sn")[:], sm("cs")[:],
                            (tsc_i[:, 0:64], tsc_f[:, 0:64], tsc_g[:, 0:64]), btsc, R_, W_)
            V_tt('vector', sm("abre")[:], sm("mag")[:], sm("cs")[:], ALU.mult, reads=R_, writes=W_)
            V_tt('vector', sm("abim")[:], sm("mag")[:], sm("sn")[:], ALU.mult, reads=R_, writes=W_)
            V_tt('vector', sm("den")[:], sm("lr")[:], sm("lr")[:], ALU.mult, reads=R_, writes=W_)
            V_tt('vector', sm("t2")[:], sm("li")[:], sm("li")[:], ALU.mult, reads=R_, writes=W_)
            V_tt('vector', sm("den")[:], sm("den")[:], sm("t2")[:], ALU.add, reads=R_, writes=W_)
            S.do('vector', lambda e: e.reciprocal(out=sm("rden")[:], in_=sm("den")[:]), reads=R_, writes=W_)
            V_ts('vector', sm("nr")[:], sm("abre")[:], -1.0, None, ALU.add, reads=R_, writes=W_)
            V_tt('vector', sm("u1")[:], sm("nr")[:], sm("lr")[:], ALU.mult, reads=R_, writes=W_)
            V_tt('vector', sm("u2")[:], sm("abim")[:], sm("li")[:], ALU.mult, reads=R_, writes=W_)
            V_tt('vector', sm("u1")[:], sm("u1")[:], sm("u2")[:], ALU.add, reads=R_, writes=W_)
            V_tt('vector', sm("cfr")[:], sm("u1")[:], sm("rden")[:], ALU.mult, reads=R_, writes=W_)
            V_tt('vector', sm("u1")[:], sm("abim")[:], sm("lr")[:], ALU.mult, reads=R_, writes=W_)
            V_tt('vector', sm("u2")[:], sm("nr")[:], sm("li")[:], ALU.mult, reads=R_, writes=W_)
            V_tt('vector', sm("u1")[:], sm("u1")[:], sm("u2")[:], ALU.subtract, reads=R_, writes=W_)
            V_tt('vector', sm("cfi")[:], sm("u1")[:], sm("rden")[:], ALU.mult, reads=R_, writes=W_)

            def bc16(t):
                return t[:].unsqueeze(2).to_broadcast([128, 64, 16])

            def v3(ap_):
                return ap_.rearrange("p (a h) -> p a h", h=16)

            def cmul(eng, o_re, o_im, a_re, a_im, lr_, li_, t1, t2, bb):
                R2 = [bb, bs]
                V_tt(eng, v3(t1), v3(a_re), lr_, ALU.mult, reads=R2, writes=[bb])
                V_tt(eng, v3(t2), v3(a_im), li_, ALU.mult, reads=R2, writes=[bb])
                V_tt(eng, o_re, t1, t2, ALU.subtract, reads=R2, writes=[bb])
                V_tt(eng, v3(t1), v3(a_re), li_, ALU.mult, reads=R2, writes=[bb])
                V_tt(eng, v3(t2), v3(a_im), lr_, ALU.mult, reads=R2, writes=[bb])
                V_tt(eng, o_im, t1, t2, ALU.add, reads=R2, writes=[bb])

            cmul('vector', CURR, CURI, BRE, BIM, bc16(sm("cfr")), bc16(sm("cfi")), TB1, TB2, bB)
            cur, nxt = (CURR, CURI), (NXR, NXI)
            for k in range(8):
                for d in range(2):
                    j = (7 - k) if d == 0 else k
                    sl = slice(d * 512, (d + 1) * 512)
                    A_act(BLt[0:64, d, :, j, :], v3(cur[0][0:64, sl]), AF.Copy, reads=[bB], writes=[bT])
                    A_act(BLt[64:128, d, :, j, :], v3(cur[1][64:128, sl]), AF.Copy, reads=[bB], writes=[bT])
                if k < 7:
                    cmul('vector', nxt[0], nxt[1], cur[0], cur[1], bc16(sm("abre")), bc16(sm("abim")), TB1, TB2, bB)
                    cur, nxt = nxt, cur
            for j in range(8):
                bk = j % 2
                S.do('tensor', lambda e, j=j, bk=bk: e.transpose(PB[bk][:, 0:128], cnr[:, j, :, :].rearrange("p u q -> p (u q)"), ident_f[:]),
                     reads=[bC, bconst], writes=[bPB[bk]])
                V_cp('vector', CRE[:, j * 128:(j + 1) * 128], PB[bk][:, 0:128], reads=[bPB[bk]], writes=[bC])
                S.do('tensor', lambda e, j=j, bk=bk: e.transpose(PB[2 + bk][:, 0:128], cni[:, j, :, :].rearrange("p u q -> p (u q)"), ident_f[:]),
                     reads=[bC, bconst], writes=[bPB[2 + bk]])
                V_cp('vector', CIM[:, j * 128:(j + 1) * 128], PB[2 + bk][:, 0:128], reads=[bPB[2 + bk]], writes=[bC])
            cur = (CRE, CIM)
            cbufs = [(CCR, CCI), (CNR, CNI)]
            for k in range(9):
                for d in range(2):
                    sl = slice(d * 512, (d + 1) * 512)
                    if k == 0:
                        o1lo, o1hi = CL0[0:64, d, :, :], CL0[64:128, d, :, :]
                    else:
                        t = (k - 1) if d == 0 else (8 - k)
                        o1lo, o1hi = WoA[0:64, d, :, t, :], WoA[64:128, d, :, t, :]
                    A_act(o1lo, v3(cur[0][0:64, sl]), AF.Copy, reads=[bC], writes=[bT])
                    S.do('scalar', lambda e, o_=o1hi, i_=v3(cur[1][64:128, sl]): e.mul(out=o_, in_=i_, mul=-1.0), reads=[bC], writes=[bT])
                    if k >= 1:
                        S.do('scalar', lambda e, o_=Wo2A[0:64, d, :, t, :], i_=v3(cur[1][0:64, sl]): e.mul(out=o_, in_=i_, mul=-1.0), reads=[bC], writes=[bT])
                        A_act(Wo2A[64:128, d, :, t, :], v3(cur[0][64:128, sl]), AF.Copy, reads=[bC], writes=[bT])
                if k < 8:
                    nx = cbufs[k % 2]
                    cmul('vector', nx[0], nx[1], cur[0], cur[1], bc16(sm("abre")), bc16(sm("abim")), TC1, TC2, bC)
                    cur = nx
            kall = a1f[0:16, 0:3840].bitcast(BF16)[:, 0:7680].rearrange("p (g c) -> p g c", c=240)
            dskb = wsl[0][0][0:16, :, :].rearrange("p a b -> p (a b)").bitcast(F32)
            dsk2 = wsl[1][0][0:16, :, :].rearrange("p a b -> p (a b)").bitcast(F32).rearrange("p (g h) -> p g h", h=16)
            S.dmado('sync', dskb[:], ssm_d.broadcast_to([16, 512]), 'prep_k', writes=[bs])
            V_tt('vector', dsk2[:], dskb[:].rearrange("p (g h) -> p g h", h=16),
                 dmask[:].unsqueeze(1).to_broadcast([16, 32, 16]), ALU.mult, reads=[bs, bconst], writes=[bs])
            bK = Buf()
            for blk in range(4):
                def mmk(e, blk=blk):
                    ins = None
                    for g8 in range(8):
                        g = blk * 8 + g8
                        bank = PB[g8 // 2]
                        off = (g8 % 2) * 256
                        lf = BLt[:, 0, g, 7, :]
                        lb = BLt[:, 1, g, 0, :]
                        e.matmul(bank[0:16, off:off + 112], lb, WoA[:, 1, g, 1:8, :].rearrange("p j h -> p (j h)"), start=True, stop=True, skip_group_check=True)
                        e.matmul(bank[0:16, off + 112:off + 128], lf, CL0[:, 0, g, :], start=True, stop=False, skip_group_check=True)
                        e.matmul(bank[0:16, off + 112:off + 128], lb, CL0[:, 1, g, :], start=False, stop=True, skip_group_check=True)
                        ins = e.matmul(bank[0:16, off + 128:off + 240], lf, WoA[:, 0, g, 0:7, :].rearrange("p j h -> p (j h)"), start=True, stop=True, skip_group_check=True)
                    return ins
                S.do('tensor', mmk, reads=[bT, bB], writes=[bPB[0], bPB[1], bPB[2], bPB[3]])
                for g8 in range(8):
                    g = blk * 8 + g8
                    bank = PB[g8 // 2]
                    off = (g8 % 2) * 256
                    V_cp('vector', kall[:, g, :], bank[0:16, off:off + 240], reads=[bPB[g8 // 2]], writes=[bK, bB])
                    V_tt('vector', kall[:, g, 112:128], bank[0:16, off + 112:off + 128], dsk2[:, g, :], ALU.add,
                         reads=[bPB[g8 // 2], bs], writes=[bK])
            T_all = a2f[:, 0:2048].bitcast(BF16)[:, 0:4096].rearrange("p (g c) -> p g c", c=128)
            Ws_all = a2f[:, 2048:6144].bitcast(BF16)[:, 0:8192].rearrange("p (g d c) -> p g d c", d=2, c=128)
            bTa, bWs = Buf(), Buf()
            for s in range(8):
                S.dmado('sync', T_all[16 * s:16 * s + 16, :, :], kall[:, :, (7 - s) * 16:(7 - s) * 16 + 128], 'prep_t',
                        reads=[bK], writes=[bTa, bC])
            for g in range(32):
                for d in range(2):
                    bk = 4 + ((g * 2 + d) % 2)
                    pbb = PB[bk][:, 0:64].bitcast(BF16)
                    S.do('tensor', lambda e, g=g, d=d, pbb=pbb: e.transpose(pbb, BLt[:, d, g, :, :].rearrange("p j h -> p (j h)"), ident_b[:]),
                         reads=[bT, bconst], writes=[bPB[bk]])
                    if d == 0:
                        V_cp('vector', Ws_all[:, g, d, :], pbb, reads=[bPB[bk]], writes=[bWs, bC])
                    else:
                        A_act(Ws_all[:, g, d, :], pbb, AF.Copy, reads=[bPB[bk]], writes=[bWs, bC])
            sw = ssmw.rearrange("g p (s c) -> p g s c", c=128)
            S.dmado('sync', sw[:, :, 0, :], T_all, 'prep', reads=[bTa])
            for d in range(2):
                S.dmado('sync', sw[:, :, 1 + d, :], Ws_all[:, :, d, :], 'prep', reads=[bWs])
                S.dmado('sync', sw[:, :, 3 + d, :], WoA[:, d, :, :, :].rearrange("p g j h -> p g (j h)"), 'prep', reads=[bT])
                S.dmado('sync', sw[:, :, 5 + d, :], Wo2A[:, d, :, :, :].rearrange("p g j h -> p g (j h)"), 'prep', reads=[bT])
            hfin = (('d', 'prep'), S.cnt[('d', 'prep')])
            for b in (bA1, bA2, bA3, bA4, bWB, stg[0][1], stg[1][1], wsl[0][1], wsl[1][1], wsl[2][1], xs_pool[1][1]):
                b.w = hfin
            return hfin

        h_prep = ssm_prep()
        ssm_post_prep = True
        bssmw = Buf()
        bssmw.w = h_prep

        wctr = [0, 0]

        def stage_load(src3, kc, ncol):
            i = wctr[0] % NSTG
            wctr[0] += 1
            t, b = stg[i]
            S.dmado('sync', t[:, 0:kc, 0:ncol], src3, 'stg%d' % i, writes=[b])
            return t, b

        def wload(src3, kc, ncol):
            t, b = stage_load(src3, kc, ncol)
            j = wctr[1] % NWS
            wctr[1] += 1
            w, bw = wsl[j]
            V_cp('gpsimd', w[:, 0:kc, 0:ncol], t[:, 0:kc, 0:ncol], reads=[b], writes=[bw])
            return w, bw

        def win_cols(c0, ncol=128):
            return w_in[:, c0:c0 + ncol].rearrange("(k p) n -> p k n", p=128)

        import collections
        seq_order = ([b * 128 for b in range(8)] + [1024 + b * 128 for b in range(4)] + [1536 + b * 128 for b in range(4)]
                     + [3072 + b * 128 for b in range(8)] + [2048 + b * 128 for b in range(4)] + [2560 + b * 128 for b in range(4)]
                     + [4096 + b * 128 for b in range(8)])
        wq_order = seq_order + seq_order
        wq_loaded = collections.deque()
        wq_idx = [0]
        LOOK = 2

        def next_w(c0):
            while len(wq_loaded) < LOOK + 1 and wq_idx[0] < len(wq_order):
                cc = wq_order[wq_idx[0]]
                wq_idx[0] += 1
                wq_loaded.append((cc, wload(win_cols(cc), 8, 128)))
            cc, r = wq_loaded.popleft()
            assert cc == c0, (cc, c0)
            return r

        cTt = sb("cTt", [128, 8, 2])
        sc = sb("sc", [128, 8, 2])
        screp2 = [xt_pool[0][0][:, :].rearrange("p (k n) -> p k n", k=8), xt_pool[1][0][:, :].rearrange("p (k n) -> p k n", k=8)]
        badaT = sb("badaT", [128, 24])
        gpreT = sb("gpreT", [128, 8])
        bada_bc = xs_pool[0][0]
        S.dmado('sync', cTt[:], cT.rearrange("(k p) b -> p k b", p=128), 'p0', writes=[bmod])
        S.dmado('sync', badaT[:], b_ada.rearrange("o (k p) -> p (o k)", p=128), 'p0', writes=[bmod])
        S.dmado('sync', gpreT[:], g_pre.rearrange("o (k p) -> p (o k)", p=128), 'p0', writes=[bmod])
        S.dmado('sync', bglu_c[:], b_glu.rearrange("o (k p) -> p (o k)", p=128), 'p0', writes=[bmod])
        S.dmado('sync', bada_bc[:], b_ada[:, 2 * D:3 * D].broadcast_to([128, D]), 'p0', writes=[bmod, xs_pool[0][1]])
        S.dmado('sync', gfin_bc[:], g_final.broadcast_to([128, D]), 'p0', writes=[bmod])
        S.dmado('sync', gs08[:], g_subln, 'p0', writes=[bmod])
        lq = sb("lq", [128, 256])
        S.dmado('sync', lq[:], lam_qk.broadcast_to([128, 256]), 'p0', writes=[bmod])
        A_act(sc[:], cTt[:], AF.Silu, reads=[bmod], writes=[bmod])
        for b in range(2):
            V_cp('vector', screp2[b][:, :, :], sc[:, :, b:b + 1].to_broadcast([128, 8, 128]), reads=[bmod], writes=[bmod, xt_pool[b][1]])
        lq2 = sb("lq2", [128, 2, 64])
        lqs = sb("lqs", [128, 2])
        V_tt('vector', lq2[:], lq[:].rearrange("p (a b c) -> p a b c", a=2, b=2)[:, :, 0, :],
             lq[:].rearrange("p (a b c) -> p a b c", a=2, b=2)[:, :, 1, :], ALU.mult, reads=[bmod], writes=[bmod])
        S.do('vector', lambda e: e.tensor_reduce(out=lqs[:], in_=lq2[:], axis=AX.X, op=ALU.add), reads=[bmod], writes=[bmod])
        A_act(lqs[:], lqs[:], AF.Exp, reads=[bmod], writes=[bmod])
        V_tt('vector', lamneg[:], lqs[:, 1:2], lqs[:, 0:1], ALU.subtract, reads=[bmod], writes=[bmod])
        V_ts('vector', lamneg[:], lamneg[:], -0.2, None, ALU.add, reads=[bmod], writes=[bmod])
        V_ts('vector', gs08[:], gs08[:], 0.8, None, ALU.mult, reads=[bmod], writes=[bmod])
        for grp in range(16):
            t, bt = stage_load(w_ada[:, grp * 128:(grp + 1) * 128].rearrange("(k p) n -> p k n", p=128), 8, 128)

            def mm0(e, t=t, grp=grp):
                ins = None
                for k in range(8):
                    ins = e.matmul(PB[grp % 2][:, 0:2], t[:, k, :], sc[:, k, :], start=(k == 0), stop=(k == 7))
                return ins
            S.do('tensor', mm0, reads=[bt, bmod], writes=[bPB[grp % 2]])
            V_ts('vector', modT[:, grp, :], PB[grp % 2][:, 0:2], badaT[:, grp:grp + 1], None, ALU.add,
                 reads=[bPB[grp % 2], bmod], writes=[bmod])
        for grp in range(8):
            t, bt = stage_load(w_ada[:, 2 * D + grp * 128:2 * D + (grp + 1) * 128].rearrange("(k p) n -> p k n", p=128), 8, 128)
            for b in range(2):
                bk = (grp * 2 + b) % 2

                def mm1(e, t=t, b=b, bk=bk):
                    ins = None
                    for k in range(8):
                        ins = e.matmul(PB[bk][:, 0:128], screp2[b][:, k, :], t[:, k, :], start=(k == 0), stop=(k == 7))
                    return ins
                S.do('tensor', mm1, reads=[bt, bmod, xt_pool[b][1]], writes=[bPB[bk]])
                V_tt('vector', gate_bc[:, b, grp * 128:(grp + 1) * 128], PB[bk][:, 0:128], bada_bc[:, grp * 128:(grp + 1) * 128],
                     ALU.add, reads=[bPB[bk], bmod, xs_pool[0][1]], writes=[bmod])
        V_ts('vector', a_col[:], modT[:, 8:16, :], 1.0, None, ALU.add, reads=[bmod], writes=[bmod])
        V_tt('vector', a_col[:], a_col[:], gpreT[:].unsqueeze(2).to_broadcast([128, 8, 2]), ALU.mult, reads=[bmod], writes=[bmod])
        V_cp('vector', s_col[:], modT[:, 0:8, :], reads=[bmod], writes=[bmod])
        if at('p0'):
            dump('modT', modT[:], [bmod])
            dump('a_col', a_col[:], [bmod])
            dump('gate_bc', gate_bc[:], [bmod])
            dump('lamneg', lamneg[:], [bmod])
            dump('rho', rho[:], [brho])
            dump('phq', phq[:], [brho])
            dump('ssmw', ssmw, [bssmw])

        hT = A1[:, :].rearrange("p (k t) -> p k t", k=8)
        qT = A2[:, 0:8192].rearrange("p (b t) -> p b t", b=4)
        kT = A2[:, 8192:16384].rearrange("p (b t) -> p b t", b=4)
        MT = A2[:, :].rearrange("p (m t) -> p m t", m=8)
        vaug = A3[:, 0:16 * 4 * 132].rearrange("p (t h e) -> p t h e", t=16, h=4)
        YT = A3[:, 0:8192].rearrange("p (j t) -> p j t", j=4)
        a4f = A4[:, :].bitcast(F32)
        ropeC = a4f[:, 0:2048]
        ropeS = a4f[:, 2048:4096]
        On = A4[:, :].rearrange("p (t c) -> p t c", t=16)
        uT = A4[:, :].rearrange("p (b t) -> p b t", b=4)

        ss_pool = tmp_pool("ss", 2, [128, 2])
        f32_pool = tmp_pool("f5", 1, [128, 512]) + [(tsc_f2[:, :, :].rearrange("p a b -> p (a b)"), btsc)]
        bf_pool = tmp_pool("b5", 3, [128, 512], BF16)
        et_pool = tmp_pool("et", 2, [128, 512], BF16)
        o0n_pool = tmp_pool("o0n", 1, [128, 4, 128])
        otmp_pool = tmp_pool("otmp", 2, [128, 128])
        rz_pool = tmp_pool("rz", 2, [128, 4])
        ctr = {}

        def nxt_(name, pool):
            i = ctr.get(name, 0)
            ctr[name] = i + 1
            return pool[i % len(pool)]

        def inproj_fm(c0, tb, bank):
            w, bw = inproj_fm.cur

            def mm(e, w=w, tb=tb, bank=bank):
                ins = None
                for k in range(8):
                    ins = e.matmul(PB[bank][:, :], w[:, k, :], hT[:, k, tb * 512:(tb + 1) * 512], start=(k == 0), stop=(k == 7))
                return ins
            S.do('tensor', mm, reads=[bw, bA1], writes=[bPB[bank]])

        def seq_pipeline(sq):
            posi = A3[:, 0:4096].bitcast(I32)
            posf = A3[:, 4096:8192].bitcast(F32)
            S.dmado('sync', posi, pos[sq:sq + 1, :].broadcast_to([128, SEQ]), 'pos', writes=[bA3])
            V_cp('vector', posf, posi, reads=[bA3], writes=[bA3])
            V_ts('vector', posf, posf, invf[:, 0:1], None, ALU.mult, reads=[bA3, bconst], writes=[bA3])
            for q8 in range(8):
                sl = slice(q8 * 256, (q8 + 1) * 256)
                turns_to_sincos('vector', posf[:, sl], ropeS[:, sl], ropeC[:, sl],
                                (tsc_i[:], tsc_f[:], tsc_g[:]), btsc, [bA3], [bA4])
            if at('rope'):
                dump('ropeC', ropeC, [bA4])
                dump('ropeS', ropeS, [bA4])
                raise StopBuild()
            for tt in range(16):
                xt, bxt = nxt_("xt", xt_pool)
                xs, bxs_ = nxt_("xs", xs_pool)
                ss, bss = nxt_("ss", ss_pool)
                S.dmado('sync', xt[:], x[sq, tt * 128:(tt + 1) * 128, :], 'xt%d' % (ctr["xt"] % 2), writes=[bxt])
                A_act(xs[:], xt[:], AF.Square, reads=[bxt], writes=[bxs_, bss], accum_out=ss[:, 0:1])
                V_ts('vector', ss[:, 1:2], ss[:, 0:1], 1.0 / D, EPS, ALU.mult, ALU.add, reads=[bss], writes=[bss])
                V_tt('gpsimd', ss[:, 1:2], ss[:, 1:2], negh[:, 0:1], ALU.pow, reads=[bss, bconst], writes=[bss])
                V_ts('vector', xs[:], xt[:], ss[:, 1:2], None, ALU.mult, reads=[bxt, bss], writes=[bxs_])
                for half in range(2):
                    bank = 6 + half

                    def tp(e, xs=xs, half=half, bank=bank):
                        ins = None
                        for kk in range(4):
                            k = half * 4 + kk
                            ins = e.transpose(PB[bank][:, kk * 128:(kk + 1) * 128], xs[:, k * 128:(k + 1) * 128], ident_f[:])
                        return ins
                    S.do('tensor', tp, reads=[bxs_, bconst], writes=[bPB[bank]])
                    for kk in range(4):
                        k = half * 4 + kk
                        A_act(hT[:, k, tt * 128:(tt + 1) * 128], PB[bank][:, kk * 128:(kk + 1) * 128], AF.Identity,
                              reads=[bPB[bank], bmod], writes=[bA1], scale=a_col[:, k, sq:sq + 1], bias=s_col[:, k, sq:sq + 1])
            if at('p1'):
                dump('hT', hT, [bA1])
                raise StopBuild()
            for blk in range(8):
                inproj_fm.cur = next_w(blk * 128)
                dst = qT if blk < 4 else kT
                for tb in range(4):
                    bank = (blk * 4 + tb) % 2
                    inproj_fm(blk * 128, tb, bank)
                    qb_, bqb = nxt_("b5", bf_pool)
                    A_act(qb_[:], PB[bank][:, :], AF.Copy, reads=[bPB[bank]], writes=[bqb])
                    if at('qk_a'):
                        dump('qb', qb_[:], [bqb])
                        raise StopBuild()
                    pbk = 2 + bank
                    S.do('tensor', lambda e, qb_=qb_, pbk=pbk: e.matmul(PB[pbk][:, :], permT[:], qb_[:], start=True, stop=True),
                         reads=[bqb, bconst], writes=[bPB[pbk]])
                    t1, bt1 = nxt_("f5", f32_pool)
                    t2, bt2 = nxt_("f5", f32_pool)
                    sl = slice(tb * 512, (tb + 1) * 512)
                    if at('qk_b1'):
                        A_act(t1[:], PB[pbk][:, :], AF.Copy, reads=[bPB[pbk]], writes=[bt1])
                        dump('t1', t1[:], [bt1])
                        raise StopBuild()
                    if at('qk_b3'):
                        V_cp('vector', t1[:], PB[bank][:, :], reads=[bPB[bank]], writes=[bt1])
                        dump('t1', t1[:], [bt1])
                        raise StopBuild()
                    if at('qk_b4'):
                        V_tt('vector', t1[:], qb_[:], ropeC[:, sl], ALU.mult, reads=[bqb, bA4], writes=[bt1])
                        dump('t1', t1[:], [bt1])
                        raise StopBuild()
                    if at('qk_b2'):
                        V_tt('vector', t1[:], PB[bank][:, :], ropeC[:, sl], ALU.mult, reads=[bPB[bank], bA4], writes=[bt1])
                        dump('t1', t1[:], [bt1])
                        raise StopBuild()
                    V_tt('vector', t1[:], PB[bank][:, :], ropeC[:, sl], ALU.mult, reads=[bPB[bank], bA4], writes=[bt1])
                    V_tt('vector', t2[:], PB[pbk][:, :], ropeS[:, sl], ALU.mult, reads=[bPB[pbk], bA4], writes=[bt2])
                    if at('qk_b'):
                        dump('t1', t1[:], [bt1])
                        dump('t2', t2[:], [bt2])
                        raise StopBuild()
                    V_tt('vector', dst[:, blk % 4, sl], t1[:], t2[:], ALU.add, reads=[bt1, bt2], writes=[bA2])
                    if at('qk_c'):
                        dump('q0', dst[:, blk % 4, sl], [bA2])
                        raise StopBuild()
            if at('qk'):
                dump('hT', hT, [bA1])
                dump('ropeC', ropeC, [bA4])
                dump('ropeS', ropeS, [bA4])
                dump('qT', qT, [bA2])
                dump('kT', kT, [bA2])
                raise StopBuild()
            S.do('gpsimd', lambda e: e.memset(vaug[:, :, :, 128:129], 1.0), reads=[], writes=[bA3])
            for vb in range(4):
                w, bw = next_w(1024 + vb * 128)
                for tt in range(16):
                    bank = (vb * 16 + tt) % 2

                    def mmv(e, w=w, tt=tt, bank=bank):
                        ins = None
                        for k in range(8):
                            ins = e.matmul(PB[bank][:, 0:128], hT[:, k, tt * 128:(tt + 1) * 128], w[:, k, :], start=(k == 0), stop=(k == 7))
                        return ins
                    S.do('tensor', mmv, reads=[bw, bA1], writes=[bPB[bank]])
                    if tt % 2 == 0:
                        A_act(vaug[:, tt, vb, 0:128], PB[bank][:, 0:128], AF.Copy, reads=[bPB[bank]], writes=[bA3])
                    else:
                        V_cp('vector', vaug[:, tt, vb, 0:128], PB[bank][:, 0:128], reads=[bPB[bank]], writes=[bA3])
            if at('v'):
                dump('vaug', vaug, [bA3])
                raise StopBuild()
            wupa = WB[:, 0:4096].rearrange("p (j n) -> p j n", j=4)
            for cg in range(8):
                t, bt = stage_load(w_up_attn[:, cg * 128:(cg + 1) * 128].rearrange("(k p) n -> p k n", p=128), 4, 128)
                V_ts('gpsimd', wupa[:, :, cg * 128:(cg + 1) * 128], t[:, 0:4, :], gs08[:, 0:1], None, ALU.mult,
                     reads=[bt, bmod], writes=[bWB])
            steps = [(hd, qb, c, kt) for hd in range(4) for qb in range(4) for c in range(2) for kt in range(16)]
            bOn = bA4

            qm = [bf_pool[0], bf_pool[1]]
            S.do('gpsimd', lambda e: e.memset(qm[0][0][64:128, :], 0.0), writes=[qm[0][1]])
            S.do('gpsimd', lambda e: e.memset(qm[1][0][0:64, :], 0.0), writes=[qm[1][1]])

            f5v = f32_pool[0][0][:, :].bitcast(BF16)
            ets = [et_pool[0], et_pool[1], (f5v[:, 0:512], f32_pool[0][1]), (f5v[:, 512:1024], f32_pool[0][1])]

            def issue_S(i):
                hd, qb, c, kt = steps[i]
                bank = i % 4
                qmt, bqm = qm[c]
                if kt == 0:
                    V_cp('gpsimd', qmt[c * 64:(c + 1) * 64, :], qT[c * 64:(c + 1) * 64, hd, qb * 512:(qb + 1) * 512], reads=[bA2], writes=[bqm])
                S.do('tensor', lambda e: e.matmul(PB[bank][:, :], kT[:, hd, kt * 128:(kt + 1) * 128], qmt[:, :], start=True, stop=True),
                     reads=[bA2, bqm], writes=[bPB[bank]])
                et, bet = ets[i % 4]
                A_act(et[:], PB[bank][:, :], AF.Exp, reads=[bPB[bank]], writes=[bet], scale=0.125)

            def issue_PV(i):
                hd, qb, c, kt = steps[i]
                et, bet = ets[i % 4]

                ab = 4 + 2 * ((i // 16) % 2)

                def mm(e):
                    ins = None
                    for qt in range(4):
                        bank = ab + qt // 2
                        off = (qt % 2) * 256
                        ins = e.matmul(PB[bank][:, off:off + 129], et[:, qt * 128:(qt + 1) * 128], vaug[:, kt, hd, 0:129],
                                       start=(kt == 0 and qt % 2 == 0), stop=(kt == 15), skip_group_check=True)
                    return ins
                S.do('tensor', mm, reads=[bet, bA3], writes=[bPB[ab], bPB[ab + 1]])
                if kt == 15:
                    rz, brz = nxt_("rz", rz_pool)
                    o0n, bo0 = o0n_pool[0]
                    for qt in range(4):
                        bank = ab + qt // 2
                        off = (qt % 2) * 256
                        tt = qb * 4 + qt
                        S.do('vector', lambda e, bank=bank, off=off, qt=qt, rz=rz: e.reciprocal(out=rz[:, qt:qt + 1], in_=PB[bank][:, off + 128:off + 129]),
                             reads=[bPB[bank]], writes=[brz])
                        if c == 0:
                            V_ts('vector', o0n[:, qt, :], PB[bank][:, off:off + 128], rz[:, qt:qt + 1], None, ALU.mult,
                                 reads=[bPB[bank], brz], writes=[bo0])
                        else:
                            ot, bot = nxt_("otmp", otmp_pool)
                            ss, bss = nxt_("ss", ss_pool)
                            V_tt('vector', rz[:, qt:qt + 1], rz[:, qt:qt + 1], lamneg[:, 0:1], ALU.mult, reads=[brz, bmod], writes=[brz])
                            S.do('vector', lambda e, bank=bank, off=off, qt=qt, rz=rz, ot=ot, o0n=o0n: e.scalar_tensor_tensor(
                                out=ot[:], in0=PB[bank][:, off:off + 128], scalar=rz[:, qt:qt + 1], in1=o0n[:, qt, :],
                                op0=ALU.mult, op1=ALU.add), reads=[bPB[bank], brz, bo0], writes=[bot])
                            jk, bjk = bf_pool[2]
                            A_act(jk[:, 0:128], ot[:], AF.Square, reads=[bot], writes=[bjk, bss], accum_out=ss[:, 0:1])
                            V_ts('vector', ss[:, 1:2], ss[:, 0:1], 1.0 / 128.0, EPS, ALU.mult, ALU.add, reads=[bss], writes=[bss])
                            V_tt('gpsimd', ss[:, 1:2], ss[:, 1:2], negh[:, 0:1], ALU.pow, reads=[bss, bconst], writes=[bss])
                            V_ts('vector', On[:, tt, hd * 128:(hd + 1) * 128], ot[:], ss[:, 1:2], None, ALU.mult,
                                 reads=[bot, bss], writes=[bOn])
            for i in range(3):
                issue_S(i)
            for i in range(len(steps)):
                if i + 3 < len(steps):
                    issue_S(i + 3)
                issue_PV(i)
            if at('attn'):
                dump('vaug', vaug, [bA3])
                dump('On', On, [bA4])
                raise StopBuild()
            for j in range(4):
                inproj_fm.cur = next_w(1536 + j * 128)
                for tb in range(4):
                    bank = (j * 4 + tb) % 2
                    inproj_fm(0, tb, bank)
                    sz, bsz = nxt_("b5", bf_pool)
                    A_act(sz[:], PB[bank][:, :], AF.Silu, reads=[bPB[bank]], writes=[bsz])
                    tbk = 6 + (j * 4 + tb) % 2
                    ptb = PB[tbk][:, 0:256].bitcast(BF16)

                    def tp2(e, j=j, tb=tb, ptb=ptb):
                        ins = None
                        for q in range(4):
                            ins = e.transpose(ptb[:, q * 128:(q + 1) * 128], On[:, tb * 4 + q, j * 128:(j + 1) * 128], ident_b[:])
                        return ins
                    S.do('tensor', tp2, reads=[bA4, bconst], writes=[bPB[tbk]])
                    V_tt('vector', YT[:, j, tb * 512:(tb + 1) * 512], ptb, sz[:], ALU.mult, reads=[bPB[tbk], bsz], writes=[bA3])
            if at('yt'):
                dump('YT', YT, [bA3])
                raise StopBuild()
            for m in range(8):
                inproj_fm.cur = next_w(3072 + m * 128)
                for tb in range(4):
                    bank = (m * 4 + tb) % 2
                    inproj_fm(0, tb, bank)
                    sg, bsg = nxt_("b5", bf_pool)
                    A_act(sg[:], PB[bank][:, :], AF.Sigmoid, reads=[bPB[bank]], writes=[bsg])
                    ub = 2 + bank

                    def mmu(e, m=m, tb=tb, ub=ub):
                        ins = None
                        for j in range(4):
                            ins = e.matmul(PB[ub][:, :], wupa[:, j, m * 128:(m + 1) * 128], YT[:, j, tb * 512:(tb + 1) * 512],
                                           start=(j == 0), stop=(j == 3))
                        return ins
                    S.do('tensor', mmu, reads=[bWB, bA3], writes=[bPB[ub]])
                    V_tt('vector', MT[:, m, tb * 512:(tb + 1) * 512], PB[ub][:, :], sg[:], ALU.mult, reads=[bPB[ub], bsg], writes=[bA2])
            if at('mt'):
                dump('YT', YT, [bA3])
                dump('MT', MT, [bA2])
                raise StopBuild()
            ssm_seq(sq)
            if at('ssm'):
                dump('yT', YT, [bA3])
                dump('y2T', uT, [bA4])
                dump('MT', MT, [bA2])
                raise StopBuild()
            wo = WB[:, :].rearrange("p (k n) -> p k n", k=8)
            for cg in range(8):
                t, bt = stage_load(w_out[:, cg * 128:(cg + 1) * 128].rearrange("(k p) n -> p k n", p=128), 8, 128)
                V_cp('gpsimd', wo[:, :, cg * 128:(cg + 1) * 128], t[:, :, :], reads=[bt], writes=[bWB])
            for tt in range(16):
                xt, bxt = nxt_("xt", xt_pool)
                xs, bxs_ = nxt_("xs", xs_pool)
                ss, bss = nxt_("ss", ss_pool)
                S.dmado('sync', xt[:], x[sq, tt * 128:(tt + 1) * 128, :], 'xt%d' % (ctr["xt"] % 2), writes=[bxt])
                for half in range(2):
                    bank = half

                    def mmo(e, tt=tt, half=half, bank=bank):
                        ins = None
                        for k in range(8):
                            ins = e.matmul(PB[bank][:, :], MT[:, k, tt * 128:(tt + 1) * 128], wo[:, k, half * 512:(half + 1) * 512],
                                           start=(k == 0), stop=(k == 7))
                        return ins
                    S.do('tensor', mmo, reads=[bA2, bWB], writes=[bPB[bank]])
                    sl = slice(half * 512, (half + 1) * 512)
                    V_tt('vector', xs[:, sl], PB[bank][:, :], gate_bc[:, sq, sl], ALU.mult, reads=[bPB[bank], bmod], writes=[bxs_])
                    V_tt('gpsimd', xt[:, sl], xt[:, sl], xs[:, sl], ALU.add, reads=[bxs_], writes=[bxt])
                A_act(xs[:], xt[:], AF.Square, reads=[bxt], writes=[bxs_, bss], accum_out=ss[:, 0:1])
                V_ts('vector', ss[:, 1:2], ss[:, 0:1], 1.0 / D, EPS, ALU.mult, ALU.add, reads=[bss], writes=[bss])
                V_tt('gpsimd', ss[:, 1:2], ss[:, 1:2], negh[:, 0:1], ALU.pow, reads=[bss, bconst], writes=[bss])
                S.do('vector', lambda e, xt=xt, ss=ss, xs=xs: e.scalar_tensor_tensor(out=xs[:], in0=xt[:], scalar=ss[:, 1:2], in1=gfin_bc[:],
                                                                                    op0=ALU.mult, op1=ALU.mult),
                     reads=[bxt, bss, bmod], writes=[bxs_])
                S.dmado('sync', out[sq, tt * 128:(tt + 1) * 128, :], xs[:], 'outst%d' % (ctr["xs"] % 2), reads=[bxs_])

        sw_pool = tmp_pool("sw", 3, [128, 7, 128], BF16)
        vg_pool = tmp_pool("vg", 3, [128, 256], BF16)
        tq2_pool = tmp_pool("tq2", 2, [128, 2, 256])
        phq_s = sb("phq_s", [128, 64])
        tab_pool = [(sb("tab%d" % i, [128, 2, 256], BF16), Buf()) for i in range(4)]
        zz_pool = tmp_pool("zz", 2, [128, 256])
        zs_pool = tmp_pool("zs", 2, [128, 256])
        z2_pool = tmp_pool("z2", 2, [128, 256])
        dd_pool = tmp_pool("dd", 4, [128, 2, 256], BF16)
        yg_pool = tmp_pool("yg", 1, [128, 8, 256], BF16)
        for i_, (dd_, bdd_) in enumerate(dd_pool):
            col = 0 if i_ % 2 == 0 else 255
            S.do('gpsimd', lambda e, dd_=dd_, col=col: e.memset(dd_[:, :, col:col + 1], 0.0), writes=[bdd_])
        S.do('vector', lambda e: e.tensor_scalar(out=phq_s[:], in0=phq[:], scalar1=sgn2pi[:, 0:1], scalar2=1.0 / TWO_PI,
                                                 op0=ALU.mult, op1=ALU.mult), reads=[brho, bconst], writes=[brho])

        def ssm_seq(sq):
            for b4 in range(4):
                inproj_fm.cur = next_w(2048 + b4 * 128)
                for tb in range(4):
                    bank = (b4 * 4 + tb) % 2
                    inproj_fm(0, tb, bank)
                    if tb % 2 == 0:
                        A_act(uT[:, b4, tb * 512:(tb + 1) * 512], PB[bank][:, :], AF.Copy, reads=[bPB[bank]], writes=[bA4])
                    else:
                        V_cp('vector', uT[:, b4, tb * 512:(tb + 1) * 512], PB[bank][:, :], reads=[bPB[bank]], writes=[bA4])
            if at('ssm_u'):
                dump('uT', uT, [bA4])
                raise StopBuild()
            wups = WB[:, 0:4096].rearrange("p (j n) -> p j n", j=4)
            wglu = WB[:, 4096:6144].rearrange("p (j n) -> p j n", j=4)
            for cg in range(8):
                t, bt = stage_load(w_up_ssm[:, cg * 128:(cg + 1) * 128].rearrange("(k p) n -> p k n", p=128), 4, 128)
                V_cp('gpsimd', wups[:, :, cg * 128:(cg + 1) * 128], t[:, 0:4, :], reads=[bt], writes=[bWB])
            for cg in range(4):
                t, bt = stage_load(w_glu[:, cg * 128:(cg + 1) * 128].rearrange("(k p) n -> p k n", p=128), 4, 128)
                V_cp('gpsimd', wglu[:, :, cg * 128:(cg + 1) * 128], t[:, 0:4, :], reads=[bt], writes=[bWB])
            yT = YT
            SBK = [6, 5]

            def stageA(g):
                blk, g8 = g // 8, g % 8
                swt, bsw = nxt_("sw", sw_pool)
                S.dmado('sync', swt[:], ssmw[g].rearrange("p (s c) -> p s c", c=128), 'sw%d' % (ctr["sw"] % len(sw_pool)),
                        reads=[bssmw], writes=[bsw])
                vg, bvg = nxt_("vg", vg_pool)

                def mmv(e, g8=g8, blk=blk):
                    ins = None
                    for s_ in range(8):
                        ins = e.matmul(PB[7][:, 0:256], rtab[:, g8, 112 - 16 * s_:240 - 16 * s_],
                                       uT[:, blk, s_:SEQ:8], start=(s_ == 0), stop=(s_ == 7))
                    return ins
                S.do('tensor', mmv, reads=[bA4, bconst], writes=[bPB[7]])
                A_act(vg[:], PB[7][:, 0:256], AF.Copy, reads=[bPB[7]], writes=[bvg])
                return dict(g=g, swt=swt, bsw=bsw, vg=vg, bvg=bvg)

            def stageB(G):
                swt, bsw, vg, bvg = G['swt'], G['bsw'], G['vg'], G['bvg']
                for d in range(2):
                    bk = SBK[d]

                    def mm2(e, swt=swt, d=d, vg=vg, bk=bk):
                        e.matmul(PB[bk][:, 0:256], swt[:, 1 + d, :], vg[:], start=True, stop=True, skip_group_check=True)
                        e.matmul(PB[bk][64:128, 256:512], swt[:, 1 + d, 0:64], vg[:], start=True, stop=True, skip_group_check=True)
                        return e.matmul(PB[bk][0:64, 256:512], swt[:, 1 + d, 64:128], vg[:], start=True, stop=True, skip_group_check=True)
                    S.do('tensor', mm2, reads=[bsw, bvg], writes=[bPB[bk]])

            def tables(g):
                res = []
                for d in range(2):
                    gd = d * 32 + g
                    tab, btab = nxt_("tab", tab_pool)
                    tq2, btq = nxt_("tq2", tq2_pool)
                    A_act(tq2[:, 0, :], iota_c[:], AF.Identity, reads=[bconst, brho], writes=[btq], scale=phq[:, gd:gd + 1], bias=qtr[:, 0:1])
                    A_act(tq2[:, 1, :], iota_c[:], AF.Identity, reads=[bconst, brho], writes=[btq], scale=phq_s[:, gd:gd + 1])
                    V_cp('gpsimd', tsc_i2[:], tq2[:], reads=[btq], writes=[btsc])
                    V_cp('vector', tsc_f2[:], tsc_i2[:], reads=[btsc], writes=[btsc])
                    V_tt('vector', tq2[:], tq2[:], tsc_f2[:], ALU.subtract, reads=[btsc], writes=[btq])
                    A_act(tab[:], tq2[:], AF.Sin, reads=[btq], writes=[btab], scale=TWO_PI)
                    res.append((tab, btab))
                return res

            def chain(G):
                g = G['g']
                tabs = G['tabs']
                dds = []
                zzs = []
                for d in range(2):
                    bk = SBK[d]
                    tab, btab = tabs[d]
                    rv = (lambda a_: a_) if d == 0 else (lambda a_: a_[:, ::-1])
                    zz, bzz = nxt_("zz", zz_pool)
                    z2, bz2 = nxt_("z2", z2_pool)
                    V_tt('vector', zz[:], rv(PB[bk][:, 0:256]), tab[:, 0, :], ALU.mult, reads=[bPB[bk], btab], writes=[bzz])
                    V_tt('vector', z2[:], rv(PB[bk][:, 256:512]), tab[:, 1, :], ALU.mult, reads=[bPB[bk], btab], writes=[bz2])
                    V_tt('vector', zz[:], zz[:], z2[:], ALU.add, reads=[bz2], writes=[bzz])
                    zzs.append((zz, bzz))
                for d in range(2):
                    gd = d * 32 + g
                    tab, btab = tabs[d]
                    rv = (lambda a_: a_) if d == 0 else (lambda a_: a_[:, ::-1])
                    zz, bzz = zzs[d]
                    zs, bzs = nxt_("zs", zs_pool)
                    S.do('vector', lambda e, zs=zs, zz=zz, gd=gd: e.tensor_tensor_scan(
                        out=zs[:], data0=rho[:, gd:gd + 1].to_broadcast([128, 256]), data1=zz[:], initial=0.0,
                        op0=ALU.mult, op1=ALU.add), reads=[bzz, brho], writes=[bzs])
                    dd, bdd = dd_pool[(2 * ctr.get("ddg", 0) + d) % 4]
                    for a_ in range(2):
                        dv = rv(dd[:, a_, :])
                        V_tt('gpsimd', dv[:, 1:256], zs[:, 0:255], tab[:, a_, 0:255], ALU.mult, reads=[bzs, btab], writes=[bdd])
                    dds.append((dd, bdd))
                ctr["ddg"] = ctr.get("ddg", 0) + 1
                G['dds'] = dds

            def stageC(G, ygA, bygA):
                swt, bsw, vg, bvg, dds = G['swt'], G['bsw'], G['vg'], G['bvg'], G['dds']

                def mmy(e, swt=swt, vg=vg, dds=dds):
                    e.matmul(PB[4][:, 0:256], swt[:, 0, :], vg[:], start=True, stop=False)
                    ins = None
                    for d in range(2):
                        dd = dds[d][0]
                        e.matmul(PB[4][:, 0:256], swt[:, 3 + d, :], dd[:, 0, :], start=False, stop=False)
                        ins = e.matmul(PB[4][:, 0:256], swt[:, 5 + d, :], dd[:, 1, :], start=False, stop=(d == 1))
                    return ins
                S.do('tensor', mmy, reads=[bsw, bvg, dds[0][1], dds[1][1]], writes=[bPB[4]])
                V_cp('vector', ygA[:, G['g'] % 8, :], PB[4][:, 0:256], reads=[bPB[4]], writes=[bygA])

            def regroup(blk, ygA, bygA):
                for t in range(8):
                    bank = t % 2

                    def mmb(e, t=t, ygA=ygA, bank=bank):
                        ins = None
                        for g8 in range(8):
                            ins = e.matmul(PB[bank][:, 0:256], rtab[:, t, 112 - 16 * g8:240 - 16 * g8], ygA[:, g8, :],
                                           start=(g8 == 0), stop=(g8 == 7))
                        return ins
                    S.do('tensor', mmb, reads=[bygA, bconst], writes=[bPB[bank]])
                    A_act(yT[:, blk, t:SEQ:8], PB[bank][:, 0:256], AF.Gelu_apprx_tanh, reads=[bPB[bank]], writes=[bA3])

            ygs = {}
            prev = None
            nxt_tabs = tables(0)
            for g in range(32):
                G = stageA(g)
                stageB(G)
                G['tabs'] = nxt_tabs
                if g + 1 < 32:
                    nxt_tabs = tables(g + 1)
                if prev is not None:
                    pb = prev['g'] // 8
                    if pb not in ygs:
                        ygs[pb] = nxt_("yg", yg_pool)
                    stageC(prev, *ygs[pb])
                    if prev['g'] % 8 == 7:
                        regroup(pb, *ygs[pb])
                chain(G)
                prev = G
            pb = 3
            if pb not in ygs:
                ygs[pb] = nxt_("yg", yg_pool)
            stageC(prev, *ygs[pb])
            regroup(pb, *ygs[pb])
            if at('ssm_y'):
                dump('yT', yT, [bA3])
                raise StopBuild()
            for n in range(4):
                inproj_fm.cur = next_w(2560 + n * 128)
                for tb in range(4):
                    sl = slice(tb * 512, (tb + 1) * 512)
                    bank = (n * 4 + tb) % 2
                    inproj_fm(0, tb, bank)
                    sz, bsz = nxt_("b5", bf_pool)
                    A_act(sz[:], PB[bank][:, :], AF.Silu, reads=[bPB[bank]], writes=[bsz])
                    gb = 2 + bank

                    def mmg(e, n=n, sl=sl, gb=gb):
                        ins = None
                        for j in range(4):
                            ins = e.matmul(PB[gb][:, :], wglu[:, j, n * 128:(n + 1) * 128], yT[:, j, sl], start=(j == 0), stop=(j == 3))
                        return ins
                    S.do('tensor', mmg, reads=[bWB, bA3], writes=[bPB[gb]])
                    gl, bgl = nxt_("b5", bf_pool)
                    A_act(gl[:], PB[gb][:, :], AF.Sigmoid, reads=[bPB[gb], bmod], writes=[bgl], bias=bglu_c[:, n:n + 1])
                    V_tt('gpsimd', gl[:], gl[:], sz[:], ALU.mult, reads=[bsz], writes=[bgl])
                    V_tt('vector', uT[:, n, sl], yT[:, n, sl], gl[:], ALU.mult, reads=[bA3, bgl], writes=[bA4])
            y2T = uT
            for m in range(8):
                inproj_fm.cur = next_w(4096 + m * 128)
                for tb in range(4):
                    sl = slice(tb * 512, (tb + 1) * 512)
                    bank = (m * 4 + tb) % 2
                    inproj_fm(0, tb, bank)
                    sg, bsg = nxt_("b5", bf_pool)
                    A_act(sg[:], PB[bank][:, :], AF.Sigmoid, reads=[bPB[bank]], writes=[bsg])
                    ub = 2 + bank

                    def mmu(e, m=m, sl=sl, ub=ub):
                        ins = None
                        for j in range(4):
                            ins = e.matmul(PB[ub][:, :], wups[:, j, m * 128:(m + 1) * 128], y2T[:, j, sl], start=(j == 0), stop=(j == 3))
                        return ins
                    S.do('tensor', mmu, reads=[bWB, bA4], writes=[bPB[ub]])
                    t1, bt1 = nxt_("b5", bf_pool)
                    V_tt('vector', t1[:], PB[ub][:, :], sg[:], ALU.mult, reads=[bPB[ub], bsg], writes=[bt1])
                    V_tt('gpsimd', MT[:, m, sl], MT[:, m, sl], t1[:], ALU.add, reads=[bt1], writes=[bA2])

        try:
            if at('prep') or at('p0'):
                raise StopBuild()
            for sq in range(2):
                seq_pipeline(sq)
        except StopBuild:
            pass
        fin = [(k, S.cnt[k]) for k in S.dma_keys if k[1].startswith('outst') or k[1] == 'dbgst']
        S.wait_only('sync', fin)
        S.run()
    return nc


_CACHE = {}


def kernel(_stage='all', _ncores=NCORES, **inputs):
    f = lambda a: np.ascontiguousarray(np.asarray(a))
    x = f(inputs["x"]).astype(np.float32, copy=False)
    c = f(inputs["c"]).astype(np.float32, copy=False)
    positions = f(inputs["positions"]).astype(np.int32, copy=False)
    consts = host_consts()
    shared = {
        "w_ada": f(inputs["w_ada"])[0], "b_ada": f(inputs["b_ada"]).reshape(1, 3 * D),
        "g_pre": f(inputs["g_pre"]).reshape(1, D), "w_in": f(inputs["w_in"])[0],
        "lam_qk": f(inputs["lam_qk"]).reshape(1, 256), "g_subln": f(inputs["g_subln"]).reshape(128, 1),
        "lam_re": f(inputs["ssm_lam_re"]).reshape(64, 64), "lam_im": f(inputs["ssm_lam_im"]).reshape(64, 64),
        "log_dt": f(inputs["ssm_log_dt"]).reshape(1, 64),
        "b_re": f(inputs["ssm_b_re"])[0], "b_im": f(inputs["ssm_b_im"])[0],
        "c_re": f(inputs["ssm_c_re"])[0], "c_im": f(inputs["ssm_c_im"])[0],
        "ssm_d": f(inputs["ssm_d"]).reshape(1, 512), "w_glu": f(inputs["w_glu"])[0],
        "b_glu": f(inputs["b_glu"]).reshape(1, 512), "w_up_attn": f(inputs["w_up_attn"])[0],
        "w_up_ssm": f(inputs["w_up_ssm"])[0], "w_out": f(inputs["w_out"])[0],
        "g_final": f(inputs["g_final"]).reshape(1, D),
    }
    shared = {k: np.ascontiguousarray(v) for k, v in shared.items()}
    shared.update(consts)
    if _stage not in _CACHE:
        _CACHE[_stage] = build(_stage)
    nc = _CACHE[_stage]
    in_maps = []
    for i in range(_ncores):
        m = dict(shared)
        m["x"] = np.ascontiguousarray(x[2 * i:2 * i + 2])
        m["cT"] = np.ascontiguousarray(c[2 * i:2 * i + 2].T)
        m["pos"] = np.ascontiguousarray(positions[2 * i:2 * i + 2])
        in_maps.append(m)
    res = run_bass_kernel_spmd(nc, in_maps, core_ids=list(range(_ncores)))
    if _stage != 'all':
        return res.results
    outs = [np.asarray(r["out"]) for r in res.results]
    return np.concatenate(outs, axis=0).astype(np.float32, copy=False)
```

```python
import math
import contextlib
import numpy as np
import ml_dtypes
import concourse.bass as bass
import concourse.mybir as mybir
from concourse.bass_utils import run_bass_kernel_spmd

F32 = mybir.dt.float32
BF16 = mybir.dt.bfloat16
I32 = mybir.dt.int32
AF = mybir.ActivationFunctionType
ALU = mybir.AluOpType
AX = mybir.AxisListType

ENGS = ['tensor', 'vector', 'scalar', 'gpsimd', 'sync']
TWO_PI = 2.0 * math.pi
EPS = 1e-6
SEQ = 2048
D = 1024
NCORES = 8


class Buf:
    __slots__ = ('w', 'r', 'excl')

    def __init__(self, excl=False):
        self.w = None
        self.r = {}
        self.excl = excl


class Sched:
    def __init__(self, nc):
        self.nc = nc
        self.prog = {e: [] for e in ENGS}
        self.cnt = {}
        self.waited = {e: {} for e in ENGS}
        self.semh = {}
        self.dma_keys = {}

    def _waits(self, eng, deps):
        waits = []
        for d in deps:
            if d is None:
                continue
            key, val = d
            if eng == 'tensor' and key == ('e', 'tensor'):
                continue
            if self.waited[eng].get(key, 0) >= val:
                continue
            self.waited[eng][key] = val
            waits.append((key, val))
        return waits

    def op(self, eng, fn, deps=(), sig=True):
        waits = self._waits(eng, deps)
        h = None
        if sig:
            key = ('e', eng)
            self.cnt[key] = self.cnt.get(key, 0) + 1
            h = (key, self.cnt[key])

        def emit(e, waits=waits, fn=fn, h=h):
            for key, v in waits:
                e.wait_ge(self.semh[key], v)
            ins = fn(e)
            if h is not None:
                ins.then_inc(self.semh[h[0]], 1)
        self.prog[eng].append(emit)
        return h

    def dma(self, q, out, in_, key, deps=()):
        waits = self._waits(q, deps)
        k = ('d', key)
        self.dma_keys[k] = True
        self.cnt[k] = self.cnt.get(k, 0) + 16
        h = (k, self.cnt[k])

        def emit(e, waits=waits, h=h, out=out, in_=in_):
            for key_, v in waits:
                e.wait_ge(self.semh[key_], v)
            e.dma_start(out=out, in_=in_).then_inc(self.semh[h[0]], 16)
        self.prog[q].append(emit)
        return h

    @staticmethod
    def _deps(reads, writes):
        deps = []
        for b in reads:
            deps.append(b.w)
        for b in writes:
            deps.append(b.w)
            deps.extend(b.r.items())
        return deps

    @staticmethod
    def _mark(h, reads, writes):
        for b in reads:
            if h[1] > b.r.get(h[0], 0):
                b.r[h[0]] = h[1]
        for b in writes:
            b.w = h
            b.r = {}

    def do(self, eng, fn, reads=(), writes=()):
        ex = [b for b in reads if b.excl]
        if ex:
            reads = [b for b in reads if not b.excl]
            writes = list(writes) + ex
        h = self.op(eng, fn, self._deps(reads, writes), True)
        self._mark(h, reads, writes)
        return h

    def dmado(self, q, out, in_, key, reads=(), writes=()):
        h = self.dma(q, out, in_, key, self._deps(reads, writes))
        self._mark(h, reads, writes)
        return h

    def wait_only(self, eng, deps):
        waits = self._waits(eng, deps)

        def emit(e, waits=waits):
            for key, v in waits:
                e.wait_ge(self.semh[key], v)
        self.prog[eng].append(emit)

    def run(self):
        nc = self.nc
        keys = [('e', e) for e in ENGS] + list(self.dma_keys.keys())
        with contextlib.ExitStack() as st:
            for i, k in enumerate(keys):
                self.semh[k] = st.enter_context(nc.semaphore("s%d" % i))
            block = st.enter_context(nc.Block())

            @block.tensor
            def _(e):
                for f in self.prog['tensor']:
                    f(e)

            @block.vector
            def _(e):
                for f in self.prog['vector']:
                    f(e)

            @block.scalar
            def _(e):
                for f in self.prog['scalar']:
                    f(e)

            @block.gpsimd
            def _(e):
                for f in self.prog['gpsimd']:
                    f(e)

            @block.sync
            def _(e):
                for f in self.prog['sync']:
                    f(e)


def host_consts():
    c = {}
    c["ident_f"] = np.eye(128, dtype=np.float32)
    c["ident_b"] = np.eye(128, dtype=np.float32).astype(ml_dtypes.bfloat16)
    pm = np.zeros((128, 128), np.float32)
    for m in range(128):
        j = m % 64
        if j < 8:
            pm[m + 8, m] = -1.0
        elif j < 16:
            pm[m - 8, m] = 1.0
    c["permT"] = pm.astype(ml_dtypes.bfloat16)
    invf = np.zeros((128, 1), np.float32)
    for p in range(128):
        j = p % 64
        if j < 16:
            invf[p, 0] = np.float32(500000.0) ** np.float32(-(j % 8) * 2.0 / 16.0)
    c["invf"] = (invf / np.float32(TWO_PI)).astype(np.float32)
    R = np.zeros((128, 8, 240), np.float32)
    for a in range(8):
        for h in range(16):
            R[16 * a + h, a, 112 + h] = 1.0
    c["rtab"] = R.astype(ml_dtypes.bfloat16)
    c["iota_c"] = np.arange(256, dtype=np.float32)[None, :]
    dm = np.zeros((16, 16), np.float32)
    for h in range(16):
        dm[h, h] = 1.0
    c["dmask"] = dm
    sg = np.full((128, 1), TWO_PI, np.float32)
    sg[64:] = -TWO_PI
    c["sgn2pi"] = sg
    return c


class StopBuild(Exception):
    pass


def build(stage='all'):
    nc = bass.Bass("TRN2", target_bir_lowering=False)

    def din(name, shape, dt=F32):
        return nc.dram_tensor(name, list(shape), dt, kind="ExternalInput").ap()

    x = din("x", [2, SEQ, D])
    cT = din("cT", [D, 2])
    pos = din("pos", [2, SEQ], I32)
    w_ada = din("w_ada", [D, 3 * D])
    b_ada = din("b_ada", [1, 3 * D])
    g_pre = din("g_pre", [1, D])
    w_in = din("w_in", [D, 5120])
    lam_qk = din("lam_qk", [1, 256])
    g_subln = din("g_subln", [128, 1])
    lam_re = din("lam_re", [64, 64])
    lam_im = din("lam_im", [64, 64])
    log_dt = din("log_dt", [1, 64])
    b_re = din("b_re", [2, 32, 64, 16])
    b_im = din("b_im", [2, 32, 64, 16])
    c_re = din("c_re", [2, 32, 16, 64])
    c_im = din("c_im", [2, 32, 16, 64])
    ssm_d = din("ssm_d", [1, 512])
    w_glu = din("w_glu", [512, 512])
    b_glu = din("b_glu", [1, 512])
    w_up_attn = din("w_up_attn", [512, D])
    w_up_ssm = din("w_up_ssm", [512, D])
    w_out = din("w_out", [D, D])
    g_final = din("g_final", [1, D])
    k_ident_f = din("ident_f", [128, 128])
    k_ident_b = din("ident_b", [128, 128], BF16)
    k_permT = din("permT", [128, 128], BF16)
    k_invf = din("invf", [128, 1])
    k_rtab = din("rtab", [128, 8, 240], BF16)
    k_iota = din("iota_c", [1, 256])
    k_dmask = din("dmask", [16, 16])
    k_sgn = din("sgn2pi", [128, 1])
    out = nc.dram_tensor("out", [2, SEQ, D], F32, kind="ExternalOutput").ap()
    ssmw = nc.dram_tensor("ssmw", [32, 128, 7 * 128], BF16, kind="Internal").ap()
    dbg_out = {}
    order = ['prep', 'p0', 'rope', 'p1', 'qk', 'v', 'attn', 'yt', 'mt', 'ssm_u', 'ssm_y', 'ssm', 'all']

    def at(name):
        return stage == name

    S = Sched(nc)
    st = contextlib.ExitStack()
    with st:
        def sb(name, shape, dt=F32):
            return st.enter_context(nc.sbuf_tensor("sb_" + name, list(shape), dt))

        def psum(name, shape, dt=F32):
            return st.enter_context(nc.psum_tensor("ps_" + name, list(shape), dt))

        def dump(name, ap_, reads):
            t = nc.dram_tensor("dbg_" + name, list(ap_.shape), ap_.dtype, kind="ExternalOutput").ap()
            dbg_out[name] = t
            S.dmado('sync', t, ap_, 'dbgst', reads=reads)

        nc_ctx = st.enter_context(nc.allow_non_contiguous_dma(reason="small strided parameter loads"))

        A1 = sb("A1", [128, 16384], BF16)
        A2 = sb("A2", [128, 16384], BF16)
        A3 = sb("A3", [128, 8448], BF16)
        A4 = sb("A4", [128, 8192], BF16)
        WB = sb("WB", [128, 8192], BF16)
        bA1, bA2, bA3, bA4, bWB = Buf(), Buf(), Buf(), Buf(), Buf()

        PB = [psum("pb%d" % i, [128, 512]) for i in range(8)]
        bPB = [Buf(excl=True) for _ in range(8)]

        ident_f = sb("ident_f", [128, 128])
        ident_b = sb("ident_b", [128, 128], BF16)
        permT = sb("permT", [128, 128], BF16)
        invf = sb("invf", [128, 1])
        rtab = sb("rtab", [128, 8, 240], BF16)
        iota_c = sb("iota_c", [128, 256])
        dmask = sb("dmask", [16, 16])
        sgn2pi = sb("sgn2pi", [128, 1])
        bconst = Buf()
        for t, k in [(ident_f, k_ident_f), (ident_b, k_ident_b), (permT, k_permT), (invf, k_invf),
                     (rtab, k_rtab), (dmask, k_dmask), (sgn2pi, k_sgn)]:
            S.dmado('sync', t[:], k, 'const', writes=[bconst])
        S.dmado('sync', iota_c[:], k_iota.broadcast_to([128, 256]), 'const', writes=[bconst])
        negh = sb("negh", [128, 1])
        qtr = sb("qtr", [128, 1])
        S.do('gpsimd', lambda e: e.memset(qtr[:], 0.25), writes=[bconst])
        S.do('gpsimd', lambda e: e.memset(negh[:], -0.5), writes=[bconst])

        rho = sb("rho", [128, 64])
        phq = sb("phq", [128, 64])
        brho = Buf()
        modT = sb("modT", [128, 16, 2])
        a_col = sb("a_col", [128, 8, 2])
        s_col = sb("s_col", [128, 8, 2])
        gate_bc = sb("gate_bc", [128, 2, D])
        gfin_bc = sb("gfin_bc", [128, D])
        lamneg = sb("lamneg", [128, 1])
        gs08 = sb("gs08", [128, 1])
        bglu_c = sb("bglu_c", [128, 4])
        bmod = Buf()

        def tmp_pool(name, n, shape, dt=F32):
            return [(sb("%s%d" % (name, i), shape, dt), Buf()) for i in range(n)]

        NSTG, NWS = 2, 3
        stg = [(sb("stg%d" % i, [128, 8, 128]), Buf()) for i in range(NSTG)]
        wsl = [(sb("wsl%d" % i, [128, 8, 128], BF16), Buf()) for i in range(NWS)]
        xt_pool = tmp_pool("xt", 2, [128, D])
        xs_pool = tmp_pool("xs", 2, [128, D])

        def V_ts(eng, out_, in0, s1, s2, op0, op1=None, reads=(), writes=()):
            if op1 is None:
                return S.do(eng, lambda e: e.tensor_scalar(out=out_, in0=in0, scalar1=s1, scalar2=None, op0=op0), reads, writes)
            return S.do(eng, lambda e: e.tensor_scalar(out=out_, in0=in0, scalar1=s1, scalar2=s2, op0=op0, op1=op1), reads, writes)

        def V_tt(eng, out_, in0, in1, op, reads=(), writes=()):
            return S.do(eng, lambda e: e.tensor_tensor(out=out_, in0=in0, in1=in1, op=op), reads, writes)

        def V_cp(eng, out_, in_, reads=(), writes=()):
            return S.do(eng, lambda e: e.tensor_copy(out=out_, in_=in_), reads, writes)

        def A_act(out_, in_, func, reads=(), writes=(), **kw):
            return S.do('scalar', lambda e: e.activation(out=out_, in_=in_, func=func, **kw), reads, writes)

        def turns_to_sincos(eng, tq, sin_out, cos_out, scr, bscr, reads, writes, sin_scale=TWO_PI):
            (ti, fr, t2) = scr
            R = list(reads) + [bscr]
            V_cp(eng, ti, tq, reads=R, writes=[bscr])
            V_cp(eng, fr, ti, reads=R, writes=[bscr])
            V_tt(eng, fr, tq, fr, ALU.subtract, reads=R, writes=[bscr])
            A_act(sin_out, fr, AF.Sin, reads=R, writes=list(writes) + [bscr], scale=sin_scale)
            V_ts(eng, t2, tq, 0.25, None, ALU.add, reads=R, writes=[bscr])
            V_cp(eng, ti, t2, reads=R, writes=[bscr])
            V_cp(eng, fr, ti, reads=R, writes=[bscr])
            V_tt(eng, fr, t2, fr, ALU.subtract, reads=R, writes=[bscr])
            A_act(cos_out, fr, AF.Sin, reads=R, writes=list(writes) + [bscr], scale=TWO_PI)

        tsc_i2 = sb("tsc_i2", [128, 2, 256], I32)
        tsc_f2 = sb("tsc_f2", [128, 2, 256])
        tsc_i = tsc_i2[:, 0, :]
        tsc_f = tsc_f2[:, 0, :]
        tsc_g = sb("tsc_g", [128, 256])
        btsc = Buf()

        def ssm_prep():
            a1f = A1[:, :].bitcast(F32)
            a2f = A2[:, :].bitcast(F32)
            BRE, BIM = a1f[:, 0:1024], a1f[:, 1024:2048]
            CURR, CURI = a1f[:, 2048:3072], a1f[:, 3072:4096]
            NXR, NXI = a1f[:, 4096:5120], a1f[:, 5120:6144]
            TB1, TB2 = a1f[:, 6144:7168], a1f[:, 7168:8192]
            CRE, CIM = a2f[:, 0:1024], a2f[:, 1024:2048]
            CCR, CCI = a2f[:, 2048:3072], a2f[:, 3072:4096]
            CNR, CNI = a2f[:, 4096:5120], a2f[:, 5120:6144]
            TC1, TC2 = a2f[:, 6144:7168], a2f[:, 7168:8192]
            BLt = A4[:, :].rearrange("p (d g j h) -> p d g j h", d=2, g=32, j=8)
            WoA = A3[:, 0:8192].rearrange("p (d g j h) -> p d g j h", d=2, g=32, j=8)
            CL0 = xs_pool[1][0][:, 0:512].bitcast(BF16).rearrange("p (d g h) -> p d g h", d=2, g=32)
            Wo2A = WB[:, :].rearrange("p (d g j h) -> p d g j h", d=2, g=32, j=8)
            bB, bC, bT = Buf(), Buf(), Buf()
            small = {}

            stgflat = [stg[0][0][:, :, :].rearrange("p a b -> p (a b)"), stg[1][0][:, :, :].rearrange("p a b -> p (a b)")]

            class _T:
                def __init__(self, ap_):
                    self.ap_ = ap_

                def __getitem__(self, k):
                    return self.ap_[k]

            def sm(name):
                if name not in small:
                    i = len(small)
                    assert i < 32
                    small[name] = _T(stgflat[i // 16][:, (i % 16) * 64:(i % 16 + 1) * 64])
                return small[name]
            bs = Buf()
            lnat = wsl[2][0][0:64, :, :].rearrange("p a b -> p (a b)")[:, 0:512].bitcast(F32).rearrange("p (a u q) -> p a u q", a=2, u=2)
            for a, src in enumerate([lam_re, lam_im]):
                for dup in range(2):
                    S.dmado('sync', lnat[:, a, dup, :], src, 'prep_s', writes=[bs])
            S.dmado('sync', sm("ldt")[:], log_dt.broadcast_to([128, 64]), 'prep_s', writes=[bs])
            for dup in range(2):
                for d in range(2):
                    S.dmado('sync', BRE[64 * dup:64 * dup + 64, d * 512:(d + 1) * 512].rearrange("p (g h) -> p g h", h=16),
                            b_re[d].rearrange("g p h -> p g h"), 'prep_b', writes=[bB])
                    S.dmado('sync', BIM[64 * dup:64 * dup + 64, d * 512:(d + 1) * 512].rearrange("p (g h) -> p g h", h=16),
                            b_im[d].rearrange("g p h -> p g h"), 'prep_b', writes=[bB])
            cnr = TC1.rearrange("p (j u q) -> p j u q", j=8, u=2)
            cni = TC2.rearrange("p (j u q) -> p j u q", j=8, u=2)
            for dup in range(2):
                for d in range(2):
                    S.dmado('sync', cnr[:, d * 4:(d + 1) * 4, dup, :],
                            c_re[d].rearrange("(b g) h p -> (g h) b p", g=8), 'prep_c', writes=[bC])
                    S.dmado('sync', cni[:, d * 4:(d + 1) * 4, dup, :],
                            c_im[d].rearrange("(b g) h p -> (g h) b p", g=8), 'prep_c', writes=[bC])
            S.do('tensor', lambda e: e.transpose(PB[0][:, 0:64], lnat[:, 0, :, :].rearrange("p u q -> p (u q)"), ident_f[0:64, 0:64]),
                 reads=[bs, bconst], writes=[bPB[0]])
            V_cp('vector', sm("lre")[:], PB[0][:, 0:64], reads=[bPB[0]], writes=[bs])
            S.do('tensor', lambda e: e.transpose(PB[1][:, 0:64], lnat[:, 1, :, :].rearrange("p u q -> p (u q)"), ident_f[0:64, 0:64]),
                 reads=[bs, bconst], writes=[bPB[1]])
            V_cp('vector', sm("li")[:], PB[1][:, 0:64], reads=[bPB[1]], writes=[bs])
            R_, W_ = [bs], [bs]
            A_act(sm("dt")[:], sm("ldt")[:], AF.Exp, reads=R_, writes=W_)
            V_ts('vector', sm("lr")[:], sm("lre")[:], -1e-4, None, ALU.min, reads=R_, writes=W_)
            V_tt('vector', sm("x1")[:], sm("lr")[:], sm("dt")[:], ALU.mult, reads=R_, writes=W_)
            A_act(sm("mag")[:], sm("x1")[:], AF.Exp, reads=R_, writes=W_)
            A_act(rho[:], sm("x1")[:], AF.Exp, reads=R_, writes=[brho, bs], scale=8.0)
            V_tt('vector', sm("ang")[:], sm("li")[:], sm("dt")[:], ALU.mult, reads=R_, writes=W_)
            V_ts('vector', sm("tq")[:], sm("ang")[:], 1.0 / TWO_PI, None, ALU.mult, reads=R_, writes=W_)
            V_ts('vector', phq[:], sm("tq")[:], 8.0, None, ALU.mult, reads=R_, writes=[brho, bs])
            turns_to_sincos('vector', sm("tq")[:], sm("sn")[:], sm("cs")[:],
                            (tsc_i[:, 0:64], tsc_f[:, 0:64], tsc_g[:, 0:64]), btsc, R_, W_)
            V_tt('vector', sm("abre")[:], sm("mag")[:], sm("cs")[:], ALU.mult, reads=R_, writes=W_)
            V_tt('vector', sm("abim")[:], sm("mag")[:], sm("sn")[:], ALU.mult, reads=R_, writes=W_)
            V_tt('vector', sm("den")[:], sm("lr")[:], sm("lr")[:], ALU.mult, reads=R_, writes=W_)
            V_tt('vector', sm("t2")[:], sm("li")[:], sm("li")[:], ALU.mult, reads=R_, writes=W_)
            V_tt('vector', sm("den")[:], sm("den")[:], sm("t2")[:], ALU.add, reads=R_, writes=W_)
            S.do('vector', lambda e: e.reciprocal(out=sm("rden")[:], in_=sm("den")[:]), reads=R_, writes=W_)
            V_ts('vector', sm("nr")[:], sm("abre")[:], -1.0, None, ALU.add, reads=R_, writes=W_)
            V_tt('vector', sm("u1")[:], sm("nr")[:], sm("lr")[:], ALU.mult, reads=R_, writes=W_)
            V_tt('vector', sm("u2")[:], sm("abim")[:], sm("li")[:], ALU.mult, reads=R_, writes=W_)
            V_tt('vector', sm("u1")[:], sm("u1")[:], sm("u2")[:], ALU.add, reads=R_, writes=W_)
            V_tt('vector', sm("cfr")[:], sm("u1")[:], sm("rden")[:], ALU.mult, reads=R_, writes=W_)
            V_tt('vector', sm("u1")[:], sm("abim")[:], sm("lr")[:], ALU.mult, reads=R_, writes=W_)
            V_tt('vector', sm("u2")[:], sm("nr")[:], sm("li")[:], ALU.mult, reads=R_, writes=W_)
            V_tt('vector', sm("u1")[:], sm("u1")[:], sm("u2")[:], ALU.subtract, reads=R_, writes=W_)
            V_tt('vector', sm("cfi")[:], sm("u1")[:], sm("rden")[:], ALU.mult, reads=R_, writes=W_)

            def bc16(t):
                return t[:].unsqueeze(2).to_broadcast([128, 64, 16])

            def v3(ap_):
                return ap_.rearrange("p (a h) -> p a h", h=16)

            def cmul(eng, o_re, o_im, a_re, a_im, lr_, li_, t1, t2, bb):
                R2 = [bb, bs]
                V_tt(eng, v3(t1), v3(a_re), lr_, ALU.mult, reads=R2, writes=[bb])
                V_tt(eng, v3(t2), v3(a_im), li_, ALU.mult, reads=R2, writes=[bb])
                V_tt(eng, o_re, t1, t2, ALU.subtract, reads=R2, writes=[bb])
                V_tt(eng, v3(t1), v3(a_re), li_, ALU.mult, reads=R2, writes=[bb])
                V_tt(eng, v3(t2), v3(a_im), lr_, ALU.mult, reads=R2, writes=[bb])
                V_tt(eng, o_im, t1, t2, ALU.add, reads=R2, writes=[bb])

            cmul('vector', CURR, CURI, BRE, BIM, bc16(sm("cfr")), bc16(sm("cfi")), TB1, TB2, bB)
            cur, nxt = (CURR, CURI), (NXR, NXI)
            for k in range(8):
                for d in range(2):
                    j = (7 - k) if d == 0 else k
                    sl = slice(d * 512, (d + 1) * 512)
                    A_act(BLt[0:64, d, :, j, :], v3(cur[0][0:64, sl]), AF.Copy, reads=[bB], writes=[bT])
                    A_act(BLt[64:128, d, :, j, :], v3(cur[1][64:128, sl]), AF.Copy, reads=[bB], writes=[bT])
                if k < 7:
                    cmul('vector', nxt[0], nxt[1], cur[0], cur[1], bc16(sm("abre")), bc16(sm("abim")), TB1, TB2, bB)
                    cur, nxt = nxt, cur
            for j in range(8):
                bk = j % 2
                S.do('tensor', lambda e, j=j, bk=bk: e.transpose(PB[bk][:, 0:128], cnr[:, j, :, :].rearrange("p u q -> p (u q)"), ident_f[:]),
                     reads=[bC, bconst], writes=[bPB[bk]])
                V_cp('vector', CRE[:, j * 128:(j + 1) * 128], PB[bk][:, 0:128], reads=[bPB[bk]], writes=[bC])
                S.do('tensor', lambda e, j=j, bk=bk: e.transpose(PB[2 + bk][:, 0:128], cni[:, j, :, :].rearrange("p u q -> p (u q)"), ident_f[:]),
                     reads=[bC, bconst], writes=[bPB[2 + bk]])
                V_cp('vector', CIM[:, j * 128:(j + 1) * 128], PB[2 + bk][:, 0:128], reads=[bPB[2 + bk]], writes=[bC])
            cur = (CRE, CIM)
            cbufs = [(CCR, CCI), (CNR, CNI)]
            for k in range(9):
                for d in range(2):
                    sl = slice(d * 512, (d + 1) * 512)
                    if k == 0:
                        o1lo, o1hi = CL0[0:64, d, :, :], CL0[64:128, d, :, :]
                    else:
                        t = (k - 1) if d == 0 else (8 - k)
                        o1lo, o1hi = WoA[0:64, d, :, t, :], WoA[64:128, d, :, t, :]
                    A_act(o1lo, v3(cur[0][0:64, sl]), AF.Copy, reads=[bC], writes=[bT])
                    S.do('scalar', lambda e, o_=o1hi, i_=v3(cur[1][64:128, sl]): e.mul(out=o_, in_=i_, mul=-1.0), reads=[bC], writes=[bT])
                    if k >= 1:
                        S.do('scalar', lambda e, o_=Wo2A[0:64, d, :, t, :], i_=v3(cur[1][0:64, sl]): e.mul(out=o_, in_=i_, mul=-1.0), reads=[bC], writes=[bT])
                        A_act(Wo2A[64:128, d, :, t, :], v3(cur[0][64:128, sl]), AF.Copy, reads=[bC], writes=[bT])
                if k < 8:
                    nx = cbufs[k % 2]
                    cmul('vector', nx[0], nx[1], cur[0], cur[1], bc16(sm("abre")), bc16(sm("abim")), TC1, TC2, bC)
                    cur = nx
            kall = a1f[0:16, 0:3840].bitcast(BF16)[:, 0:7680].rearrange("p (g c) -> p g c", c=240)
            dskb = wsl[0][0][0:16, :, :].rearrange("p a b -> p (a b)").bitcast(F32)
            dsk2 = wsl[1][0][0:16, :, :].rearrange("p a b -> p (a b)").bitcast(F32).rearrange("p (g h) -> p g h", h=16)
            S.dmado('sync', dskb[:], ssm_d.broadcast_to([16, 512]), 'prep_k', writes=[bs])
            V_tt('vector', dsk2[:], dskb[:].rearrange("p (g h) -> p g h", h=16),
                 dmask[:].unsqueeze(1).to_broadcast([16, 32, 16]), ALU.mult, reads=[bs, bconst], writes=[bs])
            bK = Buf()
            for blk in range(4):
                def mmk(e, blk=blk):
                    ins = None
                    for g8 in range(8):
                        g = blk * 8 + g8
                        bank = PB[g8 // 2]
                        off = (g8 % 2) * 256
                        lf = BLt[:, 0, g, 7, :]
                        lb = BLt[:, 1, g, 0, :]
                        e.matmul(bank[0:16, off:off + 112], lb, WoA[:, 1, g, 1:8, :].rearrange("p j h -> p (j h)"), start=True, stop=True, skip_group_check=True)
                        e.matmul(bank[0:16, off + 112:off + 128], lf, CL0[:, 0, g, :], start=True, stop=False, skip_group_check=True)
                        e.matmul(bank[0:16, off + 112:off + 128], lb, CL0[:, 1, g, :], start=False, stop=True, skip_group_check=True)
                        ins = e.matmul(bank[0:16, off + 128:off + 240], lf, WoA[:, 0, g, 0:7, :].rearrange("p j h -> p (j h)"), start=True, stop=True, skip_group_check=True)
                    return ins
                S.do('tensor', mmk, reads=[bT, bB], writes=[bPB[0], bPB[1], bPB[2], bPB[3]])
                for g8 in range(8):
                    g = blk * 8 + g8
                    bank = PB[g8 // 2]
                    off = (g8 % 2) * 256
                    V_cp('vector', kall[:, g, :], bank[0:16, off:off + 240], reads=[bPB[g8 // 2]], writes=[bK, bB])
                    V_tt('vector', kall[:, g, 112:128], bank[0:16, off + 112:off + 128], dsk2[:, g, :], ALU.add,
                         reads=[bPB[g8 // 2], bs], writes=[bK])
            T_all = a2f[:, 0:2048].bitcast(BF16)[:, 0:4096].rearrange("p (g c) -> p g c", c=128)
            Ws_all = a2f[:, 2048:6144].bitcast(BF16)[:, 0:8192].rearrange("p (g d c) -> p g d c", d=2, c=128)
            bTa, bWs = Buf(), Buf()
            for s in range(8):
                S.dmado('sync', T_all[16 * s:16 * s + 16, :, :], kall[:, :, (7 - s) * 16:(7 - s) * 16 + 128], 'prep_t',
                        reads=[bK], writes=[bTa, bC])
            for g in range(32):
                for d in range(2):
                    bk = 4 + ((g * 2 + d) % 2)
                    pbb = PB[bk][:, 0:64].bitcast(BF16)
                    S.do('tensor', lambda e, g=g, d=d, pbb=pbb: e.transpose(pbb, BLt[:, d, g, :, :].rearrange("p j h -> p (j h)"), ident_b[:]),
                         reads=[bT, bconst], writes=[bPB[bk]])
                    if d == 0:
                        V_cp('vector', Ws_all[:, g, d, :], pbb, reads=[bPB[bk]], writes=[bWs, bC])
                    else:
                        A_act(Ws_all[:, g, d, :], pbb, AF.Copy, reads=[bPB[bk]], writes=[bWs, bC])
            sw = ssmw.rearrange("g p (s c) -> p g s c", c=128)
            S.dmado('sync', sw[:, :, 0, :], T_all, 'prep', reads=[bTa])
            for d in range(2):
                S.dmado('sync', sw[:, :, 1 + d, :], Ws_all[:, :, d, :], 'prep', reads=[bWs])
                S.dmado('sync', sw[:, :, 3 + d, :], WoA[:, d, :, :, :].rearrange("p g j h -> p g (j h)"), 'prep', reads=[bT])
                S.dmado('sync', sw[:, :, 5 + d, :], Wo2A[:, d, :, :, :].rearrange("p g j h -> p g (j h)"), 'prep', reads=[bT])
            hfin = (('d', 'prep'), S.cnt[('d', 'prep')])
            for b in (bA1, bA2, bA3, bA4, bWB, stg[0][1], stg[1][1], wsl[0][1], wsl[1][1], wsl[2][1], xs_pool[1][1]):
                b.w = hfin
            return hfin

        h_prep = ssm_prep()
        ssm_post_prep = True
        bssmw = Buf()
        bssmw.w = h_prep

        wctr = [0, 0]

        def stage_load(src3, kc, ncol):
            i = wctr[0] % NSTG
            wctr[0] += 1
            t, b = stg[i]
            S.dmado('sync', t[:, 0:kc, 0:ncol], src3, 'stg%d' % i, writes=[b])
            return t, b

        def wload(src3, kc, ncol):
            t, b = stage_load(src3, kc, ncol)
            j = wctr[1] % NWS
            wctr[1] += 1
            w, bw = wsl[j]
            V_cp('gpsimd', w[:, 0:kc, 0:ncol], t[:, 0:kc, 0:ncol], reads=[b], writes=[bw])
            return w, bw

        def win_cols(c0, ncol=128):
            return w_in[:, c0:c0 + ncol].rearrange("(k p) n -> p k n", p=128)

        import collections
        seq_order = ([b * 128 for b in range(8)] + [1024 + b * 128 for b in range(4)] + [1536 + b * 128 for b in range(4)]
                     + [3072 + b * 128 for b in range(8)] + [2048 + b * 128 for b in range(4)] + [2560 + b * 128 for b in range(4)]
                     + [4096 + b * 128 for b in range(8)])
        wq_order = seq_order + seq_order
        wq_loaded = collections.deque()
        wq_idx = [0]
        LOOK = 2

        def next_w(c0):
            while len(wq_loaded) < LOOK + 1 and wq_idx[0] < len(wq_order):
                cc = wq_order[wq_idx[0]]
                wq_idx[0] += 1
                wq_loaded.append((cc, wload(win_cols(cc), 8, 128)))
            cc, r = wq_loaded.popleft()
            assert cc == c0, (cc, c0)
            return r

        cTt = sb("cTt", [128, 8, 2])
        sc = sb("sc", [128, 8, 2])
        screp2 = [xt_pool[0][0][:, :].rearrange("p (k n) -> p k n", k=8), xt_pool[1][0][:, :].rearrange("p (k n) -> p k n", k=8)]
        badaT = sb("badaT", [128, 24])
        gpreT = sb("gpreT", [128, 8])
        bada_bc = xs_pool[0][0]
        S.dmado('sync', cTt[:], cT.rearrange("(k p) b -> p k b", p=128), 'p0', writes=[bmod])
        S.dmado('sync', badaT[:], b_ada.rearrange("o (k p) -> p (o k)", p=128), 'p0', writes=[bmod])
        S.dmado('sync', gpreT[:], g_pre.rearrange("o (k p) -> p (o k)", p=128), 'p0', writes=[bmod])
        S.dmado('sync', bglu_c[:], b_glu.rearrange("o (k p) -> p (o k)", p=128), 'p0', writes=[bmod])
        S.dmado('sync', bada_bc[:], b_ada[:, 2 * D:3 * D].broadcast_to([128, D]), 'p0', writes=[bmod, xs_pool[0][1]])
        S.dmado('sync', gfin_bc[:], g_final.broadcast_to([128, D]), 'p0', writes=[bmod])
        S.dmado('sync', gs08[:], g_subln, 'p0', writes=[bmod])
        lq = sb("lq", [128, 256])
        S.dmado('sync', lq[:], lam_qk.broadcast_to([128, 256]), 'p0', writes=[bmod])
        A_act(sc[:], cTt[:], AF.Silu, reads=[bmod], writes=[bmod])
        for b in range(2):
            V_cp('vector', screp2[b][:, :, :], sc[:, :, b:b + 1].to_broadcast([128, 8, 128]), reads=[bmod], writes=[bmod, xt_pool[b][1]])
        lq2 = sb("lq2", [128, 2, 64])
        lqs = sb("lqs", [128, 2])
        V_tt('vector', lq2[:], lq[:].rearrange("p (a b c) -> p a b c", a=2, b=2)[:, :, 0, :],
             lq[:].rearrange("p (a b c) -> p a b c", a=2, b=2)[:, :, 1, :], ALU.mult, reads=[bmod], writes=[bmod])
        S.do('vector', lambda e: e.tensor_reduce(out=lqs[:], in_=lq2[:], axis=AX.X, op=ALU.add), reads=[bmod], writes=[bmod])
        A_act(lqs[:], lqs[:], AF.Exp, reads=[bmod], writes=[bmod])
        V_tt('vector', lamneg[:], lqs[:, 1:2], lqs[:, 0:1], ALU.subtract, reads=[bmod], writes=[bmod])
        V_ts('vector', lamneg[:], lamneg[:], -0.2, None, ALU.add, reads=[bmod], writes=[bmod])
        V_ts('vector', gs08[:], gs08[:], 0.8, None, ALU.mult, reads=[bmod], writes=[bmod])
        for grp in range(16):
            t, bt = stage_load(w_ada[:, grp * 128:(grp + 1) * 128].rearrange("(k p) n -> p k n", p=128), 8, 128)

            def mm0(e, t=t, grp=grp):
                ins = None
                for k in range(8):
                    ins = e.matmul(PB[grp % 2][:, 0:2], t[:, k, :], sc[:, k, :], start=(k == 0), stop=(k == 7))
                return ins
            S.do('tensor', mm0, reads=[bt, bmod], writes=[bPB[grp % 2]])
            V_ts('vector', modT[:, grp, :], PB[grp % 2][:, 0:2], badaT[:, grp:grp + 1], None, ALU.add,
                 reads=[bPB[grp % 2], bmod], writes=[bmod])
        for grp in range(8):
            t, bt = stage_load(w_ada[:, 2 * D + grp * 128:2 * D + (grp + 1) * 128].rearrange("(k p) n -> p k n", p=128), 8, 128)
            for b in range(2):
                bk = (grp * 2 + b) % 2

                def mm1(e, t=t, b=b, bk=bk):
                    ins = None
                    for k in range(8):
                        ins = e.matmul(PB[bk][:, 0:128], screp2[b][:, k, :], t[:, k, :], start=(k == 0), stop=(k == 7))
                    return ins
                S.do('tensor', mm1, reads=[bt, bmod, xt_pool[b][1]], writes=[bPB[bk]])
                V_tt('vector', gate_bc[:, b, grp * 128:(grp + 1) * 128], PB[bk][:, 0:128], bada_bc[:, grp * 128:(grp + 1) * 128],
                     ALU.add, reads=[bPB[bk], bmod, xs_pool[0][1]], writes=[bmod])
        V_ts('vector', a_col[:], modT[:, 8:16, :], 1.0, None, ALU.add, reads=[bmod], writes=[bmod])
        V_tt('vector', a_col[:], a_col[:], gpreT[:].unsqueeze(2).to_broadcast([128, 8, 2]), ALU.mult, reads=[bmod], writes=[bmod])
        V_cp('vector', s_col[:], modT[:, 0:8, :], reads=[bmod], writes=[bmod])
        if at('p0'):
            dump('modT', modT[:], [bmod])
            dump('a_col', a_col[:], [bmod])
            dump('gate_bc', gate_bc[:], [bmod])
            dump('lamneg', lamneg[:], [bmod])
            dump('rho', rho[:], [brho])
            dump('phq', phq[:], [brho])
            dump('ssmw', ssmw, [bssmw])

        hT = A1[:, :].rearrange("p (k t) -> p k t", k=8)
        qT = A2[:, 0:8192].rearrange("p (b t) -> p b t", b=4)
        kT = A2[:, 8192:16384].rearrange("p (b t) -> p b t", b=4)
        MT = A2[:, :].rearrange("p (m t) -> p m t", m=8)
        vaug = A3[:, 0:16 * 4 * 132].rearrange("p (t h e) -> p t h e", t=16, h=4)
        YT = A3[:, 0:8192].rearrange("p (j t) -> p j t", j=4)
        a4f = A4[:, :].bitcast(F32)
        ropeC = a4f[:, 0:2048]
        ropeS = a4f[:, 2048:4096]
        On = A4[:, :].rearrange("p (t c) -> p t c", t=16)
        uT = A4[:, :].rearrange("p (b t) -> p b t", b=4)

        ss_pool = tmp_pool("ss", 2, [128, 2])
        f32_pool = tmp_pool("f5", 1, [128, 512]) + [(tsc_f2[:, :, :].rearrange("p a b -> p (a b)"), btsc)]
        bf_pool = tmp_pool("b5", 3, [128, 512], BF16)
        et_pool = tmp_pool("et", 2, [128, 512], BF16)
        o0n_pool = tmp_pool("o0n", 1, [128, 4, 128])
        otmp_pool = tmp_pool("otmp", 2, [128, 128])
        rz_pool = tmp_pool("rz", 2, [128, 4])
        ctr = {}

        def nxt_(name, pool):
            i = ctr.get(name, 0)
            ctr[name] = i + 1
            return pool[i % len(pool)]

        def inproj_fm(c0, tb, bank):
            w, bw = inproj_fm.cur

            def mm(e, w=w, tb=tb, bank=bank):
                ins = None
                for k in range(8):
                    ins = e.matmul(PB[bank][:, :], w[:, k, :], hT[:, k, tb * 512:(tb + 1) * 512], start=(k == 0), stop=(k == 7))
                return ins
            S.do('tensor', mm, reads=[bw, bA1], writes=[bPB[bank]])

        def seq_pipeline(sq):
            posi = A3[:, 0:4096].bitcast(I32)
            posf = A3[:, 4096:8192].bitcast(F32)
            S.dmado('sync', posi, pos[sq:sq + 1, :].broadcast_to([128, SEQ]), 'pos', writes=[bA3])
            V_cp('vector', posf, posi, reads=[bA3], writes=[bA3])
            V_ts('vector', posf, posf, invf[:, 0:1], None, ALU.mult, reads=[bA3, bconst], writes=[bA3])
            for q8 in range(8):
                sl = slice(q8 * 256, (q8 + 1) * 256)
                turns_to_sincos('vector', posf[:, sl], ropeS[:, sl], ropeC[:, sl],
                                (tsc_i[:], tsc_f[:], tsc_g[:]), btsc, [bA3], [bA4])
            if at('rope'):
                dump('ropeC', ropeC, [bA4])
                dump('ropeS', ropeS, [bA4])
                raise StopBuild()
            for tt in range(16):
                xt, bxt = nxt_("xt", xt_pool)
                xs, bxs_ = nxt_("xs", xs_pool)
                ss, bss = nxt_("ss", ss_pool)
                S.dmado('sync', xt[:], x[sq, tt * 128:(tt + 1) * 128, :], 'xt%d' % (ctr["xt"] % 2), writes=[bxt])
                A_act(xs[:], xt[:], AF.Square, reads=[bxt], writes=[bxs_, bss], accum_out=ss[:, 0:1])
                V_ts('vector', ss[:, 1:2], ss[:, 0:1], 1.0 / D, EPS, ALU.mult, ALU.add, reads=[bss], writes=[bss])
                V_tt('gpsimd', ss[:, 1:2], ss[:, 1:2], negh[:, 0:1], ALU.pow, reads=[bss, bconst], writes=[bss])
                V_ts('vector', xs[:], xt[:], ss[:, 1:2], None, ALU.mult, reads=[bxt, bss], writes=[bxs_])
                for half in range(2):
                    bank = 6 + half

                    def tp(e, xs=xs, half=half, bank=bank):
                        ins = None
                        for kk in range(4):
                            k = half * 4 + kk
                            ins = e.transpose(PB[bank][:, kk * 128:(kk + 1) * 128], xs[:, k * 128:(k + 1) * 128], ident_f[:])
                        return ins
                    S.do('tensor', tp, reads=[bxs_, bconst], writes=[bPB[bank]])
                    for kk in range(4):
                        k = half * 4 + kk
                        A_act(hT[:, k, tt * 128:(tt + 1) * 128], PB[bank][:, kk * 128:(kk + 1) * 128], AF.Identity,
                              reads=[bPB[bank], bmod], writes=[bA1], scale=a_col[:, k, sq:sq + 1], bias=s_col[:, k, sq:sq + 1])
            if at('p1'):
                dump('hT', hT, [bA1])
                raise StopBuild()
            for blk in range(8):
                inproj_fm.cur = next_w(blk * 128)
                dst = qT if blk < 4 else kT
                for tb in range(4):
                    bank = (blk * 4 + tb) % 2
                    inproj_fm(blk * 128, tb, bank)
                    qb_, bqb = nxt_("b5", bf_pool)
                    A_act(qb_[:], PB[bank][:, :], AF.Copy, reads=[bPB[bank]], writes=[bqb])
                    if at('qk_a'):
                        dump('qb', qb_[:], [bqb])
                        raise StopBuild()
                    pbk = 2 + bank
                    S.do('tensor', lambda e, qb_=qb_, pbk=pbk: e.matmul(PB[pbk][:, :], permT[:], qb_[:], start=True, stop=True),
                         reads=[bqb, bconst], writes=[bPB[pbk]])
                    t1, bt1 = nxt_("f5", f32_pool)
                    t2, bt2 = nxt_("f5", f32_pool)
                    sl = slice(tb * 512, (tb + 1) * 512)
                    if at('qk_b1'):
                        A_act(t1[:], PB[pbk][:, :], AF.Copy, reads=[bPB[pbk]], writes=[bt1])
                        dump('t1', t1[:], [bt1])
                        raise StopBuild()
                    if at('qk_b3'):
                        V_cp('vector', t1[:], PB[bank][:, :], reads=[bPB[bank]], writes=[bt1])
                        dump('t1', t1[:], [bt1])
                        raise StopBuild()
                    if at('qk_b4'):
                        V_tt('vector', t1[:], qb_[:], ropeC[:, sl], ALU.mult, reads=[bqb, bA4], writes=[bt1])
                        dump('t1', t1[:], [bt1])
                        raise StopBuild()
                    if at('qk_b2'):
                        V_tt('vector', t1[:], PB[bank][:, :], ropeC[:, sl], ALU.mult, reads=[bPB[bank], bA4], writes=[bt1])
                        dump('t1', t1[:], [bt1])
                        raise StopBuild()
                    V_tt('vector', t1[:], PB[bank][:, :], ropeC[:, sl], ALU.mult, reads=[bPB[bank], bA4], writes=[bt1])
                    V_tt('vector', t2[:], PB[pbk][:, :], ropeS[:, sl], ALU.mult, reads=[bPB[pbk], bA4], writes=[bt2])
                    if at('qk_b'):
                        dump('t1', t1[:], [bt1])
                        dump('t2', t2[:], [bt2])
                        raise StopBuild()
                    V_tt('gpsimd', dst[:, blk % 4, sl], t1[:], t2[:], ALU.add, reads=[bt1, bt2], writes=[bA2])
                    if at('qk_c'):
                        dump('q0', dst[:, blk % 4, sl], [bA2])
                        raise StopBuild()
            if at('qk'):
                dump('hT', hT, [bA1])
                dump('ropeC', ropeC, [bA4])
                dump('ropeS', ropeS, [bA4])
                dump('qT', qT, [bA2])
                dump('kT', kT, [bA2])
                raise StopBuild()
            S.do('gpsimd', lambda e: e.memset(vaug[:, :, :, 128:129], 1.0), reads=[], writes=[bA3])
            for vb in range(4):
                w, bw = next_w(1024 + vb * 128)
                for tt in range(16):
                    bank = (vb * 16 + tt) % 2

                    def mmv(e, w=w, tt=tt, bank=bank):
                        ins = None
                        for k in range(8):
                            ins = e.matmul(PB[bank][:, 0:128], hT[:, k, tt * 128:(tt + 1) * 128], w[:, k, :], start=(k == 0), stop=(k == 7))
                        return ins
                    S.do('tensor', mmv, reads=[bw, bA1], writes=[bPB[bank]])
                    if tt % 2 == 0:
                        A_act(vaug[:, tt, vb, 0:128], PB[bank][:, 0:128], AF.Copy, reads=[bPB[bank]], writes=[bA3])
                    else:
                        V_cp('vector', vaug[:, tt, vb, 0:128], PB[bank][:, 0:128], reads=[bPB[bank]], writes=[bA3])
            if at('v'):
                dump('vaug', vaug, [bA3])
                raise StopBuild()
            wupa = WB[:, 0:4096].rearrange("p (j n) -> p j n", j=4)
            for cg in range(8):
                t, bt = stage_load(w_up_attn[:, cg * 128:(cg + 1) * 128].rearrange("(k p) n -> p k n", p=128), 4, 128)
                V_ts('gpsimd', wupa[:, :, cg * 128:(cg + 1) * 128], t[:, 0:4, :], gs08[:, 0:1], None, ALU.mult,
                     reads=[bt, bmod], writes=[bWB])
            steps = [(hd, qb, c, kt) for hd in range(4) for qb in range(4) for c in range(2) for kt in range(16)]
            bOn = bA4

            qm = [bf_pool[0], bf_pool[1]]
            S.do('gpsimd', lambda e: e.memset(qm[0][0][64:128, :], 0.0), writes=[qm[0][1]])
            S.do('gpsimd', lambda e: e.memset(qm[1][0][0:64, :], 0.0), writes=[qm[1][1]])

            f5v = f32_pool[0][0][:, :].bitcast(BF16)
            ets = [et_pool[0], et_pool[1], (f5v[:, 0:512], f32_pool[0][1]), (f5v[:, 512:1024], f32_pool[0][1])]

            def issue_S(i):
                hd, qb, c, kt = steps[i]
                bank = i % 4
                qmt, bqm = qm[c]
                if kt == 0:
                    V_cp('gpsimd', qmt[c * 64:(c + 1) * 64, :], qT[c * 64:(c + 1) * 64, hd, qb * 512:(qb + 1) * 512], reads=[bA2], writes=[bqm])
                S.do('tensor', lambda e: e.matmul(PB[bank][:, :], kT[:, hd, kt * 128:(kt + 1) * 128], qmt[:, :], start=True, stop=True),
                     reads=[bA2, bqm], writes=[bPB[bank]])
                et, bet = ets[i % 4]
                A_act(et[:], PB[bank][:, :], AF.Exp, reads=[bPB[bank]], writes=[bet], scale=0.125)

            def issue_PV(i):
                hd, qb, c, kt = steps[i]
                et, bet = ets[i % 4]

                ab = 4 + 2 * ((i // 16) % 2)

                def mm(e):
                    ins = None
                    for qt in range(4):
                        bank = ab + qt // 2
                        off = (qt % 2) * 256
                        ins = e.matmul(PB[bank][:, off:off + 129], et[:, qt * 128:(qt + 1) * 128], vaug[:, kt, hd, 0:129],
                                       start=(kt == 0 and qt % 2 == 0), stop=(kt == 15), skip_group_check=True)
                    return ins
                S.do('tensor', mm, reads=[bet, bA3], writes=[bPB[ab], bPB[ab + 1]])
                if kt == 15:
                    rz, brz = nxt_("rz", rz_pool)
                    o0n, bo0 = o0n_pool[0]
                    for qt in range(4):
                        bank = ab + qt // 2
                        off = (qt % 2) * 256
                        tt = qb * 4 + qt
                        S.do('vector', lambda e, bank=bank, off=off, qt=qt, rz=rz: e.reciprocal(out=rz[:, qt:qt + 1], in_=PB[bank][:, off + 128:off + 129]),
                             reads=[bPB[bank]], writes=[brz])
                        if c == 0:
                            V_ts('vector', o0n[:, qt, :], PB[bank][:, off:off + 128], rz[:, qt:qt + 1], None, ALU.mult,
                                 reads=[bPB[bank], brz], writes=[bo0])
                        else:
                            ot, bot = nxt_("otmp", otmp_pool)
                            ss, bss = nxt_("ss", ss_pool)
                            V_tt('vector', rz[:, qt:qt + 1], rz[:, qt:qt + 1], lamneg[:, 0:1], ALU.mult, reads=[brz, bmod], writes=[brz])
                            S.do('vector', lambda e, bank=bank, off=off, qt=qt, rz=rz, ot=ot, o0n=o0n: e.scalar_tensor_tensor(
                                out=ot[:], in0=PB[bank][:, off:off + 128], scalar=rz[:, qt:qt + 1], in1=o0n[:, qt, :],
                                op0=ALU.mult, op1=ALU.add), reads=[bPB[bank], brz, bo0], writes=[bot])
                            jk, bjk = bf_pool[2]
                            A_act(jk[:, 0:128], ot[:], AF.Square, reads=[bot], writes=[bjk, bss], accum_out=ss[:, 0:1])
                            V_ts('vector', ss[:, 1:2], ss[:, 0:1], 1.0 / 128.0, EPS, ALU.mult, ALU.add, reads=[bss], writes=[bss])
                            V_tt('gpsimd', ss[:, 1:2], ss[:, 1:2], negh[:, 0:1], ALU.pow, reads=[bss, bconst], writes=[bss])
                            V_ts('vector', On[:, tt, hd * 128:(hd + 1) * 128], ot[:], ss[:, 1:2], None, ALU.mult,
                                 reads=[bot, bss], writes=[bOn])
            for i in range(3):
                issue_S(i)
            for i in range(len(steps)):
                if i + 3 < len(steps):
                    issue_S(i + 3)
                issue_PV(i)
            if at('attn'):
                dump('vaug', vaug, [bA3])
                dump('On', On, [bA4])
                raise StopBuild()
            for j in range(4):
                inproj_fm.cur = next_w(1536 + j * 128)
                for tb in range(4):
                    bank = (j * 4 + tb) % 2
                    inproj_fm(0, tb, bank)
                    sz, bsz = nxt_("b5", bf_pool)
                    A_act(sz[:], PB[bank][:, :], AF.Silu, reads=[bPB[bank]], writes=[bsz])
                    tbk = 6 + (j * 4 + tb) % 2
                    ptb = PB[tbk][:, 0:256].bitcast(BF16)

                    def tp2(e, j=j, tb=tb, ptb=ptb):
                        ins = None
                        for q in range(4):
                            ins = e.transpose(ptb[:, q * 128:(q + 1) * 128], On[:, tb * 4 + q, j * 128:(j + 1) * 128], ident_b[:])
                        return ins
                    S.do('tensor', tp2, reads=[bA4, bconst], writes=[bPB[tbk]])
                    V_tt('vector', YT[:, j, tb * 512:(tb + 1) * 512], ptb, sz[:], ALU.mult, reads=[bPB[tbk], bsz], writes=[bA3])
            if at('yt'):
                dump('YT', YT, [bA3])
                raise StopBuild()
            for m in range(8):
                inproj_fm.cur = next_w(3072 + m * 128)
                for tb in range(4):
                    bank = (m * 4 + tb) % 2
                    inproj_fm(0, tb, bank)
                    sg, bsg = nxt_("b5", bf_pool)
                    A_act(sg[:], PB[bank][:, :], AF.Sigmoid, reads=[bPB[bank]], writes=[bsg])
                    ub = 2 + bank

                    def mmu(e, m=m, tb=tb, ub=ub):
                        ins = None
                        for j in range(4):
                            ins = e.matmul(PB[ub][:, :], wupa[:, j, m * 128:(m + 1) * 128], YT[:, j, tb * 512:(tb + 1) * 512],
                                           start=(j == 0), stop=(j == 3))
                        return ins
                    S.do('tensor', mmu, reads=[bWB, bA3], writes=[bPB[ub]])
                    V_tt('vector', MT[:, m, tb * 512:(tb + 1) * 512], PB[ub][:, :], sg[:], ALU.mult, reads=[bPB[ub], bsg], writes=[bA2])
            if at('mt'):
                dump('YT', YT, [bA3])
                dump('MT', MT, [bA2])
                raise StopBuild()
            ssm_seq(sq)
            if at('ssm'):
                dump('yT', YT, [bA3])
                dump('y2T', uT, [bA4])
                dump('MT', MT, [bA2])
                raise StopBuild()
            wo = WB[:, :].rearrange("p (k n) -> p k n", k=8)
            for cg in range(8):
                t, bt = stage_load(w_out[:, cg * 128:(cg + 1) * 128].rearrange("(k p) n -> p k n", p=128), 8, 128)
                V_cp('gpsimd', wo[:, :, cg * 128:(cg + 1) * 128], t[:, :, :], reads=[bt], writes=[bWB])
            for tt in range(16):
                xt, bxt = nxt_("xt", xt_pool)
                xs, bxs_ = nxt_("xs", xs_pool)
                ss, bss = nxt_("ss", ss_pool)
                S.dmado('sync', xt[:], x[sq, tt * 128:(tt + 1) * 128, :], 'xt%d' % (ctr["xt"] % 2), writes=[bxt])
                for half in range(2):
                    bank = half

                    def mmo(e, tt=tt, half=half, bank=bank):
                        ins = None
                        for k in range(8):
                            ins = e.matmul(PB[bank][:, :], MT[:, k, tt * 128:(tt + 1) * 128], wo[:, k, half * 512:(half + 1) * 512],
                                           start=(k == 0), stop=(k == 7))
                        return ins
                    S.do('tensor', mmo, reads=[bA2, bWB], writes=[bPB[bank]])
                    sl = slice(half * 512, (half + 1) * 512)
                    V_tt('vector', xs[:, sl], PB[bank][:, :], gate_bc[:, sq, sl], ALU.mult, reads=[bPB[bank], bmod], writes=[bxs_])
                    V_tt('gpsimd', xt[:, sl], xt[:, sl], xs[:, sl], ALU.add, reads=[bxs_], writes=[bxt])
                A_act(xs[:], xt[:], AF.Square, reads=[bxt], writes=[bxs_, bss], accum_out=ss[:, 0:1])
                V_ts('vector', ss[:, 1:2], ss[:, 0:1], 1.0 / D, EPS, ALU.mult, ALU.add, reads=[bss], writes=[bss])
                V_tt('gpsimd', ss[:, 1:2], ss[:, 1:2], negh[:, 0:1], ALU.pow, reads=[bss, bconst], writes=[bss])
                S.do('vector', lambda e, xt=xt, ss=ss, xs=xs: e.scalar_tensor_tensor(out=xs[:], in0=xt[:], scalar=ss[:, 1:2], in1=gfin_bc[:],
                                                                                    op0=ALU.mult, op1=ALU.mult),
                     reads=[bxt, bss, bmod], writes=[bxs_])
                S.dmado('gpsimd', out[sq, tt * 128:(tt + 1) * 128, :], xs[:], 'outst%d' % (ctr["xs"] % 2), reads=[bxs_])

        sw_pool = tmp_pool("sw", 3, [128, 7, 128], BF16)
        vg_pool = tmp_pool("vg", 3, [128, 256], BF16)
        tq2_pool = tmp_pool("tq2", 2, [128, 2, 256])
        phq_s = sb("phq_s", [128, 64])
        tab_pool = [(sb("tab%d" % i, [128, 2, 256], BF16), Buf()) for i in range(4)]
        zz_pool = tmp_pool("zz", 2, [128, 256])
        zs_pool = tmp_pool("zs", 2, [128, 256])
        z2_pool = tmp_pool("z2", 2, [128, 256])
        dd_pool = tmp_pool("dd", 4, [128, 2, 256], BF16)
        yg_pool = tmp_pool("yg", 1, [128, 8, 256], BF16)
        for i_, (dd_, bdd_) in enumerate(dd_pool):
            col = 0 if i_ % 2 == 0 else 255
            S.do('gpsimd', lambda e, dd_=dd_, col=col: e.memset(dd_[:, :, col:col + 1], 0.0), writes=[bdd_])
        S.do('vector', lambda e: e.tensor_scalar(out=phq_s[:], in0=phq[:], scalar1=sgn2pi[:, 0:1], scalar2=1.0 / TWO_PI,
                                                 op0=ALU.mult, op1=ALU.mult), reads=[brho, bconst], writes=[brho])

        def ssm_seq(sq):
            for b4 in range(4):
                inproj_fm.cur = next_w(2048 + b4 * 128)
                for tb in range(4):
                    bank = (b4 * 4 + tb) % 2
                    inproj_fm(0, tb, bank)
                    if tb % 2 == 0:
                        A_act(uT[:, b4, tb * 512:(tb + 1) * 512], PB[bank][:, :], AF.Copy, reads=[bPB[bank]], writes=[bA4])
                    else:
                        V_cp('vector', uT[:, b4, tb * 512:(tb + 1) * 512], PB[bank][:, :], reads=[bPB[bank]], writes=[bA4])
            if at('ssm_u'):
                dump('uT', uT, [bA4])
                raise StopBuild()
            wups = WB[:, 0:4096].rearrange("p (j n) -> p j n", j=4)
            wglu = WB[:, 4096:6144].rearrange("p (j n) -> p j n", j=4)
            for cg in range(8):
                t, bt = stage_load(w_up_ssm[:, cg * 128:(cg + 1) * 128].rearrange("(k p) n -> p k n", p=128), 4, 128)
                V_cp('gpsimd', wups[:, :, cg * 128:(cg + 1) * 128], t[:, 0:4, :], reads=[bt], writes=[bWB])
            for cg in range(4):
                t, bt = stage_load(w_glu[:, cg * 128:(cg + 1) * 128].rearrange("(k p) n -> p k n", p=128), 4, 128)
                V_cp('gpsimd', wglu[:, :, cg * 128:(cg + 1) * 128], t[:, 0:4, :], reads=[bt], writes=[bWB])
            yT = YT
            SBK = [6, 5]

            def stageA(g):
                blk, g8 = g // 8, g % 8
                swt, bsw = nxt_("sw", sw_pool)
                S.dmado('sync', swt[:], ssmw[g].rearrange("p (s c) -> p s c", c=128), 'sw%d' % (ctr["sw"] % len(sw_pool)),
                        reads=[bssmw], writes=[bsw])
                vg, bvg = nxt_("vg", vg_pool)

                def mmv(e, g8=g8, blk=blk):
                    ins = None
                    for s_ in range(8):
                        ins = e.matmul(PB[7][:, 0:256], rtab[:, g8, 112 - 16 * s_:240 - 16 * s_],
                                       uT[:, blk, s_:SEQ:8], start=(s_ == 0), stop=(s_ == 7))
                    return ins
                S.do('tensor', mmv, reads=[bA4, bconst], writes=[bPB[7]])
                A_act(vg[:], PB[7][:, 0:256], AF.Copy, reads=[bPB[7]], writes=[bvg])
                return dict(g=g, swt=swt, bsw=bsw, vg=vg, bvg=bvg)

            def stageB(G):
                swt, bsw, vg, bvg = G['swt'], G['bsw'], G['vg'], G['bvg']
                for d in range(2):
                    bk = SBK[d]

                    def mm2(e, swt=swt, d=d, vg=vg, bk=bk):
                        e.matmul(PB[bk][:, 0:256], swt[:, 1 + d, :], vg[:], start=True, stop=True, skip_group_check=True)
                        e.matmul(PB[bk][64:128, 256:512], swt[:, 1 + d, 0:64], vg[:], start=True, stop=True, skip_group_check=True)
                        return e.matmul(PB[bk][0:64, 256:512], swt[:, 1 + d, 64:128], vg[:], start=True, stop=True, skip_group_check=True)
                    S.do('tensor', mm2, reads=[bsw, bvg], writes=[bPB[bk]])

            def tables(g):
                res = []
                for d in range(2):
                    gd = d * 32 + g
                    tab, btab = nxt_("tab", tab_pool)
                    tq2, btq = nxt_("tq2", tq2_pool)
                    A_act(tq2[:, 0, :], iota_c[:], AF.Identity, reads=[bconst, brho], writes=[btq], scale=phq[:, gd:gd + 1], bias=qtr[:, 0:1])
                    A_act(tq2[:, 1, :], iota_c[:], AF.Identity, reads=[bconst, brho], writes=[btq], scale=phq_s[:, gd:gd + 1])
                    V_cp('gpsimd', tsc_i2[:], tq2[:], reads=[btq], writes=[btsc])
                    V_cp('vector', tsc_f2[:], tsc_i2[:], reads=[btsc], writes=[btsc])
                    V_tt('vector', tq2[:], tq2[:], tsc_f2[:], ALU.subtract, reads=[btsc], writes=[btq])
                    A_act(tab[:], tq2[:], AF.Sin, reads=[btq], writes=[btab], scale=TWO_PI)
                    res.append((tab, btab))
                return res

            def chain(G):
                g = G['g']
                tabs = G['tabs']
                dds = []
                zzs = []
                for d in range(2):
                    bk = SBK[d]
                    tab, btab = tabs[d]
                    rv = (lambda a_: a_) if d == 0 else (lambda a_: a_[:, ::-1])
                    zz, bzz = nxt_("zz", zz_pool)
                    z2, bz2 = nxt_("z2", z2_pool)
                    V_tt('vector', zz[:], rv(PB[bk][:, 0:256]), tab[:, 0, :], ALU.mult, reads=[bPB[bk], btab], writes=[bzz])
                    V_tt('vector', z2[:], rv(PB[bk][:, 256:512]), tab[:, 1, :], ALU.mult, reads=[bPB[bk], btab], writes=[bz2])
                    V_tt('gpsimd', zz[:], zz[:], z2[:], ALU.add, reads=[bz2], writes=[bzz])
                    zzs.append((zz, bzz))
                for d in range(2):
                    gd = d * 32 + g
                    tab, btab = tabs[d]
                    rv = (lambda a_: a_) if d == 0 else (lambda a_: a_[:, ::-1])
                    zz, bzz = zzs[d]
                    zs, bzs = nxt_("zs", zs_pool)
                    S.do('vector', lambda e, zs=zs, zz=zz, gd=gd: e.tensor_tensor_scan(
                        out=zs[:], data0=rho[:, gd:gd + 1].to_broadcast([128, 256]), data1=zz[:], initial=0.0,
                        op0=ALU.mult, op1=ALU.add), reads=[bzz, brho], writes=[bzs])
                    dd, bdd = dd_pool[(2 * ctr.get("ddg", 0) + d) % 4]
                    for a_ in range(2):
                        dv = rv(dd[:, a_, :])
                        V_tt('gpsimd', dv[:, 1:256], zs[:, 0:255], tab[:, a_, 0:255], ALU.mult, reads=[bzs, btab], writes=[bdd])
                    dds.append((dd, bdd))
                ctr["ddg"] = ctr.get("ddg", 0) + 1
                G['dds'] = dds

            def stageC(G, ygA, bygA):
                swt, bsw, vg, bvg, dds = G['swt'], G['bsw'], G['vg'], G['bvg'], G['dds']

                def mmy(e, swt=swt, vg=vg, dds=dds):
                    e.matmul(PB[4][:, 0:256], swt[:, 0, :], vg[:], start=True, stop=False)
                    ins = None
                    for d in range(2):
                        dd = dds[d][0]
                        e.matmul(PB[4][:, 0:256], swt[:, 3 + d, :], dd[:, 0, :], start=False, stop=False)
                        ins = e.matmul(PB[4][:, 0:256], swt[:, 5 + d, :], dd[:, 1, :], start=False, stop=(d == 1))
                    return ins
                S.do('tensor', mmy, reads=[bsw, bvg, dds[0][1], dds[1][1]], writes=[bPB[4]])
                V_cp('vector', ygA[:, G['g'] % 8, :], PB[4][:, 0:256], reads=[bPB[4]], writes=[bygA])

            def regroup(blk, ygA, bygA):
                for t in range(8):
                    bank = t % 2

                    def mmb(e, t=t, ygA=ygA, bank=bank):
                        ins = None
                        for g8 in range(8):
                            ins = e.matmul(PB[bank][:, 0:256], rtab[:, t, 112 - 16 * g8:240 - 16 * g8], ygA[:, g8, :],
                                           start=(g8 == 0), stop=(g8 == 7))
                        return ins
                    S.do('tensor', mmb, reads=[bygA, bconst], writes=[bPB[bank]])
                    A_act(yT[:, blk, t:SEQ:8], PB[bank][:, 0:256], AF.Gelu_apprx_tanh, reads=[bPB[bank]], writes=[bA3])

            ygs = {}
            prev = None
            nxt_tabs = tables(0)
            for g in range(32):
                G = stageA(g)
                stageB(G)
                G['tabs'] = nxt_tabs
                if g + 1 < 32:
                    nxt_tabs = tables(g + 1)
                if prev is not None:
                    pb = prev['g'] // 8
                    if pb not in ygs:
                        ygs[pb] = nxt_("yg", yg_pool)
                    stageC(prev, *ygs[pb])
                    if prev['g'] % 8 == 7:
                        regroup(pb, *ygs[pb])
                chain(G)
                prev = G
            pb = 3
            if pb not in ygs:
                ygs[pb] = nxt_("yg", yg_pool)
            stageC(prev, *ygs[pb])
            regroup(pb, *ygs[pb])
            if at('ssm_y'):
                dump('yT', yT, [bA3])
                raise StopBuild()
            for n in range(4):
                inproj_fm.cur = next_w(2560 + n * 128)
                for tb in range(4):
                    sl = slice(tb * 512, (tb + 1) * 512)
                    bank = (n * 4 + tb) % 2
                    inproj_fm(0, tb, bank)
                    sz, bsz = nxt_("b5", bf_pool)
                    A_act(sz[:], PB[bank][:, :], AF.Silu, reads=[bPB[bank]], writes=[bsz])
                    gb = 2 + bank

                    def mmg(e, n=n, sl=sl, gb=gb):
                        ins = None
                        for j in range(4):
                            ins = e.matmul(PB[gb][:, :], wglu[:, j, n * 128:(n + 1) * 128], yT[:, j, sl], start=(j == 0), stop=(j == 3))
                        return ins
                    S.do('tensor', mmg, reads=[bWB, bA3], writes=[bPB[gb]])
                    gl, bgl = nxt_("b5", bf_pool)
                    A_act(gl[:], PB[gb][:, :], AF.Sigmoid, reads=[bPB[gb], bmod], writes=[bgl], bias=bglu_c[:, n:n + 1])
                    V_tt('gpsimd', gl[:], gl[:], sz[:], ALU.mult, reads=[bsz], writes=[bgl])
                    V_tt('vector', uT[:, n, sl], yT[:, n, sl], gl[:], ALU.mult, reads=[bA3, bgl], writes=[bA4])
            y2T = uT
            for m in range(8):
                inproj_fm.cur = next_w(4096 + m * 128)
                for tb in range(4):
                    sl = slice(tb * 512, (tb + 1) * 512)
                    bank = (m * 4 + tb) % 2
                    inproj_fm(0, tb, bank)
                    sg, bsg = nxt_("b5", bf_pool)
                    A_act(sg[:], PB[bank][:, :], AF.Sigmoid, reads=[bPB[bank]], writes=[bsg])
                    ub = 2 + bank

                    def mmu(e, m=m, sl=sl, ub=ub):
                        ins = None
                        for j in range(4):
                            ins = e.matmul(PB[ub][:, :], wups[:, j, m * 128:(m + 1) * 128], y2T[:, j, sl], start=(j == 0), stop=(j == 3))
                        return ins
                    S.do('tensor', mmu, reads=[bWB, bA4], writes=[bPB[ub]])
                    t1, bt1 = nxt_("b5", bf_pool)
                    V_tt('vector', t1[:], PB[ub][:, :], sg[:], ALU.mult, reads=[bPB[ub], bsg], writes=[bt1])
                    V_tt('gpsimd', MT[:, m, sl], MT[:, m, sl], t1[:], ALU.add, reads=[bt1], writes=[bA2])

        try:
            if at('prep') or at('p0'):
                raise StopBuild()
            for sq in range(2):
                seq_pipeline(sq)
        except StopBuild:
            pass
        fin = [(k, S.cnt[k]) for k in S.dma_keys if k[1].startswith('outst') or k[1] == 'dbgst']
        S.wait_only('sync', fin)
        S.run()
    return nc


_CACHE = {}


def kernel(_stage='all', _ncores=NCORES, **inputs):
    f = lambda a: np.ascontiguousarray(np.asarray(a))
    x = f(inputs["x"]).astype(np.float32, copy=False)
    c = f(inputs["c"]).astype(np.float32, copy=False)
    positions = f(inputs["positions"]).astype(np.int32, copy=False)
    consts = host_consts()
    shared = {
        "w_ada": f(inputs["w_ada"])[0], "b_ada": f(inputs["b_ada"]).reshape(1, 3 * D),
        "g_pre": f(inputs["g_pre"]).reshape(1, D), "w_in": f(inputs["w_in"])[0],
        "lam_qk": f(inputs["lam_qk"]).reshape(1, 256), "g_subln": f(inputs["g_subln"]).reshape(128, 1),
        "lam_re": f(inputs["ssm_lam_re"]).reshape(64, 64), "lam_im": f(inputs["ssm_lam_im"]).reshape(64, 64),
        "log_dt": f(inputs["ssm_log_dt"]).reshape(1, 64),
        "b_re": f(inputs["ssm_b_re"])[0], "b_im": f(inputs["ssm_b_im"])[0],
        "c_re": f(inputs["ssm_c_re"])[0], "c_im": f(inputs["ssm_c_im"])[0],
        "ssm_d": f(inputs["ssm_d"]).reshape(1, 512), "w_glu": f(inputs["w_glu"])[0],
        "b_glu": f(inputs["b_glu"]).reshape(1, 512), "w_up_attn": f(inputs["w_up_attn"])[0],
        "w_up_ssm": f(inputs["w_up_ssm"])[0], "w_out": f(inputs["w_out"])[0],
        "g_final": f(inputs["g_final"]).reshape(1, D),
    }
    shared = {k: np.ascontiguousarray(v) for k, v in shared.items()}
    shared.update(consts)
    if _stage not in _CACHE:
        _CACHE[_stage] = build(_stage)
    nc = _CACHE[_stage]
    in_maps = []
    for i in range(_ncores):
        m = dict(shared)
        m["x"] = np.ascontiguousarray(x[2 * i:2 * i + 2])
        m["cT"] = np.ascontiguousarray(c[2 * i:2 * i + 2].T)
        m["pos"] = np.ascontiguousarray(positions[2 * i:2 * i + 2])
        in_maps.append(m)
    res = run_bass_kernel_spmd(nc, in_maps, core_ids=list(range(_ncores)))
    if _stage != 'all':
        return res.results
    outs = [np.asarray(r["out"]) for r in res.results]
    return np.concatenate(outs, axis=0).astype(np.float32, copy=False)
```
